# Optimizing a Trainium2 kernel written in Bass

```python
import math
import jax, jax.numpy as jnp
from jax import lax
import numpy as np

D_MODEL = 1024
BATCH = 32
SEQ = 256
DEPTH = 2
DEC_BATCH = 4
DEC_SEQ = 2048
PAST_LEN = 512

GRID_W = 64
N_MIXERS = 2
N_S5_LAYERS = (DEPTH + 1) // 2
N_LRU_LAYERS = DEPTH // 2
S5_GROUP = 16
S5_GROUPS = D_MODEL // S5_GROUP
S5_STATE = 64
LRU_WIDTH = D_MODEL
LRU_BLOCKS = 16
LRU_BLOCK = LRU_WIDTH // LRU_BLOCKS
LRU_C = 8.0
CONV_W = 4
CONV_LEFT = 2
D_FF = 2816
ALPHA = (2.0 * DEPTH) ** 0.25
BETA = (8.0 * DEPTH) ** -0.25
LN_EPS = 1e-5
F32 = jnp.float32

kernel_name = "hybrid_s5_rglru_macaron_diffusion_step"


def layer_norm(x, g, b):
    xf = x.astype(F32)
    mu = jnp.mean(xf, -1, keepdims=True)
    var = jnp.mean(jnp.square(xf - mu), -1, keepdims=True)
    y = (xf - mu) * lax.rsqrt(var + LN_EPS) * g.astype(F32) + b.astype(F32)
    return y.astype(x.dtype)


def adaln_params(cond, w, b):
    m = jax.nn.silu(cond) @ w + b
    return m.reshape(cond.shape[0], 1, 3, 3, D_MODEL)


def swiglu(h, w_up, w_down):
    a, g = jnp.split(h @ w_up, 2, axis=-1)
    return (jax.nn.silu(g) * a) @ w_down


def cmul(ar, ai, br, bi):
    return ar * br - ai * bi, ar * bi + ai * br


def complex_linear_combine(e1, e2):
    a1r, a1i, b1r, b1i = e1
    a2r, a2i, b2r, b2i = e2
    ar, ai = cmul(a1r, a1i, a2r, a2i)
    br, bi = cmul(a2r, a2i, b1r, b1i)
    return ar, ai, br + b2r, bi + b2i


def real_linear_combine(e1, e2):
    return e1[0] * e2[0], e2[0] * e1[1] + e2[1]


def grid_pos_embed(rows):
    t = jnp.arange(rows * GRID_W)
    row = (t // GRID_W).astype(F32)
    col = (t % GRID_W).astype(F32)
    quarter = D_MODEL // 4
    omega = 1.0 / (10000.0 ** (jnp.arange(quarter, dtype=F32) / quarter))

    def emb(p):
        ang = p[:, None] * omega[None, :]
        return jnp.concatenate([jnp.sin(ang), jnp.cos(ang)], -1)

    return jnp.concatenate([emb(row), emb(col)], -1)


def s5_discretize(a_re, a_im, log_dt, b_re, b_im):
    dt = jnp.exp(log_dt)[:, None]
    mag = jnp.exp(dt * a_re)
    abar_re = mag * jnp.cos(dt * a_im)
    abar_im = mag * jnp.sin(dt * a_im)
    den = a_re * a_re + a_im * a_im
    nr, ni = abar_re - 1.0, abar_im
    q_re = (nr * a_re + ni * a_im) / den
    q_im = (ni * a_re - nr * a_im) / den
    bb_re, bb_im = cmul(q_re[..., None], q_im[..., None], b_re, b_im)
    return abar_re, abar_im, bb_re, bb_im


def s5_direction(u, abar_re, abar_im, bb_re, bb_im, s0, reverse):
    xr = jnp.einsum('blgi,gpi->blgp', u, bb_re)
    xi = jnp.einsum('blgi,gpi->blgp', u, bb_im)
    if s0 is not None:
        ir, ii = cmul(abar_re, abar_im, s0[0], s0[1])
        idx = -1 if reverse else 0
        xr = xr.at[:, idx].add(ir)
        xi = xi.at[:, idx].add(ii)
    ar = jnp.broadcast_to(abar_re, xr.shape)
    ai = jnp.broadcast_to(abar_im, xr.shape)
    _, _, sr, si = lax.associative_scan(complex_linear_combine, (ar, ai, xr, xi), reverse=reverse, axis=1)
    return sr, si


def s5_mixer(h, p, j, init):
    (w_in, a_re, a_im, log_dt, b_re, b_im, c_re, c_im, d_skip, w_glu) = p
    bt, l, _ = h.shape
    u = (h @ w_in[j]).astype(F32)
    ug = u.reshape(bt, l, S5_GROUPS, S5_GROUP)
    srs, sis, finals = [], [], []
    for d, rev in ((0, False), (1, True)):
        abr, abi, bbr, bbi = s5_discretize(a_re[j, d].astype(F32), a_im[j, d].astype(F32),
                                           log_dt[j, d].astype(F32), b_re[j].astype(F32), b_im[j].astype(F32))
        s0 = None if init is None else (init[:, d, 0].astype(F32), init[:, d, 1].astype(F32))
        sr, si = s5_direction(ug, abr, abi, bbr, bbi, s0, rev)
        srs.append(sr)
        sis.append(si)
        if init is None:
            k = 0 if rev else -1
            finals.append(jnp.stack([sr[:, k], si[:, k]], 1))
    s_re = srs[0] + srs[1]
    s_im = sis[0] + sis[1]
    y = (jnp.einsum('blgp,gip->blgi', s_re, c_re[j].astype(F32))
         - jnp.einsum('blgp,gip->blgi', s_im, c_im[j].astype(F32)))
    y = y.reshape(bt, l, D_MODEL) + d_skip[j].astype(F32) * u
    y = jax.nn.gelu(y).astype(h.dtype)
    a, g = jnp.split(y @ w_glu[j], 2, axis=-1)
    out = a * jax.nn.sigmoid(g)
    state = jnp.stack(finals, 1).astype(h.dtype) if init is None else None
    return out, state


def centred_dwconv(x, w, b):
    l = x.shape[1]
    xp = jnp.pad(x, ((0, 0), (CONV_LEFT, CONV_W - 1 - CONV_LEFT), (0, 0)))
    out = b + xp[:, 0:l] * w[0]
    for k in range(1, CONV_W):
        out = out + xp[:, k:k + l] * w[k]
    return out


def rglru_direction(xc, wa, ba, wx, bx, lam, h0, reverse):
    bt, l, _ = xc.shape
    xb = xc.reshape(bt, l, LRU_BLOCKS, LRU_BLOCK)
    r = jax.nn.sigmoid(jnp.einsum('blhi,hij->blhj', xb, wa).reshape(bt, l, LRU_WIDTH) + ba)
    gi = jax.nn.sigmoid(jnp.einsum('blhi,hij->blhj', xb, wx).reshape(bt, l, LRU_WIDTH) + bx)
    log_a = -LRU_C * r * jax.nn.softplus(-lam)
    a = jnp.exp(log_a)
    b = jnp.sqrt(-jnp.expm1(2.0 * log_a)) * (gi * xc)
    if h0 is not None:
        idx = -1 if reverse else 0
        b = b.at[:, idx].add(a[:, idx] * h0)
    _, hs = lax.associative_scan(real_linear_combine, (a, b), reverse=reverse, axis=1)
    return hs


def rglru_mixer(h, p, j, init):
    (w_in, conv_w, conv_b, w_a, b_a, w_x, b_x, lam, w_out) = p
    xr, gate = jnp.split(h @ w_in[j], 2, axis=-1)
    xc = centred_dwconv(xr.astype(F32), conv_w[j].astype(F32), conv_b[j].astype(F32))
    hs, finals = [], []
    for d, rev in ((0, False), (1, True)):
        h0 = None if init is None else init[:, d].astype(F32)
        hd = rglru_direction(xc, w_a[j, d].astype(F32), b_a[j, d].astype(F32), w_x[j, d].astype(F32),
                             b_x[j, d].astype(F32), lam[j, d].astype(F32), h0, rev)
        hs.append(hd)
        if init is None:
            finals.append(hd[:, 0] if rev else hd[:, -1])
    y = (hs[0] + hs[1]).astype(h.dtype) * jax.nn.gelu(gate)
    out = y @ w_out[j]
    state = jnp.stack(finals, 1).astype(h.dtype) if init is None else None
    return out, state


def trunk(x, cond, init_s5, init_lru, shared, s5p, lrup):
    ada_w, ada_b, ln_g, ln_b, ffn_w_up, ffn_w_down = shared
    states_s5, states_lru = [], []
    for i in range(DEPTH):
        mod = adaln_params(cond, ada_w[i], ada_b[i])
        sh, sc, gt = mod[:, :, 0, 0], mod[:, :, 0, 1], mod[:, :, 0, 2]
        f = swiglu(x * (1.0 + sc) + sh, ffn_w_up[i, 0], ffn_w_down[i, 0])
        x = layer_norm(ALPHA * x + 0.5 * gt * f, ln_g[i, 0], ln_b[i, 0])
        sh, sc, gt = mod[:, :, 1, 0], mod[:, :, 1, 1], mod[:, :, 1, 2]
        hm = x * (1.0 + sc) + sh
        j = i // N_MIXERS
        if i % N_MIXERS == 0:
            m, st = s5_mixer(hm, s5p, j, None if init_s5 is None else init_s5[:, j])
            if st is not None:
                states_s5.append(st)
        else:
            m, st = rglru_mixer(hm, lrup, j, None if init_lru is None else init_lru[:, j])
            if st is not None:
                states_lru.append(st)
        x = layer_norm(ALPHA * x + gt * m, ln_g[i, 1], ln_b[i, 1])
        sh, sc, gt = mod[:, :, 2, 0], mod[:, :, 2, 1], mod[:, :, 2, 2]
        f = swiglu(x * (1.0 + sc) + sh, ffn_w_up[i, 1], ffn_w_down[i, 1])
        x = layer_norm(ALPHA * x + 0.5 * gt * f, ln_g[i, 2], ln_b[i, 2])
    return x, states_s5, states_lru


def setup_inputs(seed: int = 0) -> dict:
    key = jax.random.key(seed)
    ks = jax.random.split(key, 40)
    nrm = lambda k, s, sc: jax.random.normal(k, s, F32) * sc
    D = D_MODEL
    G, P, I = S5_GROUPS, S5_STATE, S5_GROUP
    inp = {}
    inp["x_prompt"] = nrm(ks[0], (BATCH, SEQ, D), 1.0)
    inp["x_sample"] = nrm(ks[1], (DEC_BATCH, DEC_SEQ, D), 1.0)
    inp["c"] = nrm(ks[2], (DEC_BATCH, D), 1.0)
    inp["state_s5"] = nrm(ks[3], (DEC_BATCH, N_S5_LAYERS, 2, 2, G, P), 0.3)
    inp["state_lru"] = nrm(ks[4], (DEC_BATCH, N_LRU_LAYERS, 2, LRU_WIDTH), 0.5)
    inp["c_ctx"] = nrm(ks[5], (D,), 1.0)
    inp["ada_w"] = nrm(ks[6], (DEPTH, D, 9 * D), 0.5 * D ** -0.5)
    inp["ada_b"] = nrm(ks[7], (DEPTH, 9 * D), 0.02)
    inp["ln_g"] = 1.0 + nrm(ks[8], (DEPTH, 3, D), 0.02)
    inp["ln_b"] = nrm(ks[9], (DEPTH, 3, D), 0.02)
    inp["ffn_w_up"] = nrm(ks[10], (DEPTH, 2, D, 2 * D_FF), D ** -0.5)
    inp["ffn_w_down"] = nrm(ks[11], (DEPTH, 2, D_FF, D), BETA * D_FF ** -0.5)
    inp["s5_w_in"] = nrm(ks[12], (N_S5_LAYERS, D, D), D ** -0.5)
    inp["s5_a_re"] = -0.5 + nrm(ks[13], (N_S5_LAYERS, 2, G, P), 0.01)
    inp["s5_a_im"] = math.pi * jnp.arange(P, dtype=F32) + nrm(ks[14], (N_S5_LAYERS, 2, G, P), 0.01)
    inp["s5_log_dt"] = jax.random.uniform(ks[15], (N_S5_LAYERS, 2, G), F32, math.log(1e-3), math.log(1e-1))
    inp["s5_b_re"] = nrm(ks[16], (N_S5_LAYERS, G, P, I), (2.0 * I) ** -0.5)
    inp["s5_b_im"] = nrm(ks[17], (N_S5_LAYERS, G, P, I), (2.0 * I) ** -0.5)
    inp["s5_c_re"] = nrm(ks[18], (N_S5_LAYERS, G, I, P), (2.0 * P) ** -0.5)
    inp["s5_c_im"] = nrm(ks[19], (N_S5_LAYERS, G, I, P), (2.0 * P) ** -0.5)
    inp["s5_d"] = nrm(ks[20], (N_S5_LAYERS, D), 1.0)
    inp["s5_w_glu"] = nrm(ks[21], (N_S5_LAYERS, D, 2 * D), BETA * D ** -0.5)
    inp["lru_w_in"] = nrm(ks[22], (N_LRU_LAYERS, D, 2 * LRU_WIDTH), D ** -0.5)
    inp["lru_conv_w"] = nrm(ks[23], (N_LRU_LAYERS, CONV_W, LRU_WIDTH), CONV_W ** -0.5)
    inp["lru_conv_b"] = nrm(ks[24], (N_LRU_LAYERS, LRU_WIDTH), 0.02)
    inp["lru_w_a"] = nrm(ks[25], (N_LRU_LAYERS, 2, LRU_BLOCKS, LRU_BLOCK, LRU_BLOCK), LRU_BLOCK ** -0.5)
    inp["lru_b_a"] = nrm(ks[26], (N_LRU_LAYERS, 2, LRU_WIDTH), 0.02)
    inp["lru_w_x"] = nrm(ks[27], (N_LRU_LAYERS, 2, LRU_BLOCKS, LRU_BLOCK, LRU_BLOCK), LRU_BLOCK ** -0.5)
    inp["lru_b_x"] = nrm(ks[28], (N_LRU_LAYERS, 2, LRU_WIDTH), 0.02)
    u = jax.random.uniform(ks[29], (N_LRU_LAYERS, 2, LRU_WIDTH), F32, 0.9, 0.999)
    s = u ** (1.0 / LRU_C)
    inp["lru_lambda"] = jnp.log(s) - jnp.log1p(-s)
    inp["lru_w_out"] = nrm(ks[30], (N_LRU_LAYERS, LRU_WIDTH, D), BETA * LRU_WIDTH ** -0.5)
    return inp


def reference(x_prompt, x_sample, c, state_s5, state_lru, c_ctx, ada_w, ada_b, ln_g, ln_b,
              ffn_w_up, ffn_w_down, s5_w_in, s5_a_re, s5_a_im, s5_log_dt, s5_b_re, s5_b_im,
              s5_c_re, s5_c_im, s5_d, s5_w_glu, lru_w_in, lru_conv_w, lru_conv_b, lru_w_a, lru_b_a,
              lru_w_x, lru_b_x, lru_lambda, lru_w_out):
    shared = (ada_w, ada_b, ln_g, ln_b, ffn_w_up, ffn_w_down)
    s5p = (s5_w_in, s5_a_re, s5_a_im, s5_log_dt, s5_b_re, s5_b_im, s5_c_re, s5_c_im, s5_d, s5_w_glu)
    lrup = (lru_w_in, lru_conv_w, lru_conv_b, lru_w_a, lru_b_a, lru_w_x, lru_b_x, lru_lambda, lru_w_out)

    y_prompt, st_s5, st_lru = trunk(x_prompt, c_ctx[None, :], None, None, shared, s5p, lrup)
    new_state_s5 = jnp.stack(st_s5, 1)
    new_state_lru = jnp.stack(st_lru, 1)

    rows = x_sample.shape[1] // GRID_W
    xs = x_sample + grid_pos_embed(rows).astype(x_sample.dtype)[None]
    y_sample, _, _ = trunk(xs, c, state_s5, state_lru, shared, s5p, lrup)
    return (y_prompt, y_sample, new_state_s5, new_state_lru)
```

```python
import math
import numpy as np
import concourse.bass as bass
import concourse.mybir as mybir
from concourse.bass_utils import run_bass_kernel_spmd

F32 = mybir.dt.float32
BF16 = mybir.dt.bfloat16
AF = mybir.ActivationFunctionType
ALU = mybir.AluOpType

D = 1024
T = 2048
KT = 8
NBLK = 4
BLK = 512
DFF = 2816
NF = 22
ALPHA = 4.0 ** 0.25
LN_EPS = 1e-5
SECTIONS = [(0, 6), (6, 12), (12, 17), (17, 22)]
STAGE = 99


class Prog:
    def __init__(self):
        self.ops = []

    def add(self, eng, fn, r=(), w=(), stream=None):
        r2, w2 = [], []
        for k in r:
            if k[0] == 'ps':
                w2.append(('ps', k[1]))
            else:
                r2.append(k)
        for k in w:
            w2.append(('ps', k[1]) if k[0] == 'ps' else k)
        self.ops.append(dict(eng=eng, fn=fn, r=r2, w=w2, stream=stream))

    def seq(self, eng, fn, r=(), w=(), nosync=False):
        prog = self
        r = list(r)
        w = list(w)

        prog._seqid = getattr(prog, '_seqid', 0) + 1
        sid = prog._seqid if nosync else None

        class Rec:
            def __getattr__(self, name):
                def call(*a, **k):
                    prog.add(eng, lambda e: getattr(e, name)(*a, **k), r=r, w=w)
                    prog.ops[-1]['sid'] = sid
                    return None
                return call
        fn(Rec())

    def barrier(self, tile):
        self.ops.append(dict(eng='dve', fn=(lambda e: e.memset(tile, 0.0)), r=[], w=[], stream=None, barrier=True))

    def analyze(self):
        ops = self.ops
        lastw = {}
        readers = {}
        deps = [set() for _ in ops]
        last_eng = {}
        last_bar = None
        for i, op in enumerate(ops):
            if op.get('barrier'):
                for m in last_eng.values():
                    deps[i].add(m)
                last_bar = i
                last_eng = {}
            else:
                if last_bar is not None:
                    deps[i].add(last_bar)
                last_eng[(op['eng'], op['stream'])] = i
            for k in op['r']:
                if k in lastw:
                    deps[i].add(lastw[k])
            for k in op['w']:
                if k in lastw:
                    deps[i].add(lastw[k])
                for rr in readers.get(k, ()):
                    deps[i].add(rr)
            for k in op['r']:
                readers.setdefault(k, []).append(i)
            for k in op['w']:
                lastw[k] = i
                readers[k] = []
            deps[i].discard(i)
        for i, op in enumerate(ops):
            keep = set()
            for m in deps[i]:
                om = ops[m]
                if om['stream'] is None and op['stream'] is None and om['eng'] == op['eng'] == 'pe':
                    continue
                if op.get('sid') is not None and om.get('sid') == op.get('sid'):
                    continue
                keep.add(m)
            deps[i] = keep
        signaling = [False] * len(ops)
        for i in range(len(ops)):
            for m in deps[i]:
                signaling[m] = True
        cnt = {}
        sig = [None] * len(ops)
        streams = []
        for i, op in enumerate(ops):
            if op['stream'] is not None:
                s = 'dma_' + op['stream']
                if s not in cnt:
                    streams.append(s)
                cnt[s] = cnt.get(s, 0) + 16
                sig[i] = (s, cnt[s])
            elif signaling[i]:
                s = 'eng_' + op['eng']
                cnt[s] = cnt.get(s, 0) + 1
                sig[i] = (s, cnt[s])
        self.deps = deps
        self.sig = sig
        self.final = cnt
        self.streams = streams

    def emit(self, engname, e, sems, out_streams=()):
        waited = {}
        for i, op in enumerate(self.ops):
            if op['eng'] != engname:
                continue
            need = {}
            for m in self.deps[i]:
                s, v = self.sig[m]
                if need.get(s, 0) < v:
                    need[s] = v
            for s, v in need.items():
                if waited.get(s, 0) < v:
                    e.wait_ge(sems[s], v)
                    waited[s] = v
            ins = op['fn'](e)
            if self.sig[i] is not None:
                s, v = self.sig[i]
                ins.then_inc(sems[s], 16 if op['stream'] is not None else 1)
        for s in out_streams:
            e.wait_ge(sems[s], self.final[s])


def build():
    nc = bass.Bass("TRN2", target_bir_lowering=False, dynamic_dma_scratch_size=512)
    P = Prog()

    def din(name, shape, dt=F32):
        return nc.dram_tensor(name, list(shape), dt, kind="ExternalInput").ap()

    def dout(name, shape, dt=F32):
        return nc.dram_tensor(name, list(shape), dt, kind="ExternalOutput").ap()

    x_in = din("x_in", [D, T])
    cond_in = din("cond_in", [128, KT])
    flags_in = din("flags_in", [128, 4])
    maskdc_in = din("maskdc_in", [128, 256])
    s5init_in = din("s5init_in", [128, 2, 32, 2])
    lruinit_in = din("lruinit_in", [128, 2, KT])
    consts_in = din("consts_in", [128, 128 * 3 + 512 * 2])
    ada_w = din("ada_w", [2, D, 9 * D])
    ada_b = din("ada_b", [128, 2, 72])
    ln_g = din("ln_g", [128, 2, 3, KT])
    ln_b = din("ln_b", [128, 2, 3, KT])
    w_up = din("w_up", [2, 2, D, 2 * DFF])
    w_down = din("w_down", [2, 2, DFF, D])
    s5_w_in = din("s5_w_in", [D, D])
    s5_w_glu = din("s5_w_glu", [D, 2 * D])
    s5_par = din("s5_par", [128, 2, 32, 3])
    s5_bc = din("s5_bc", [128, 4, 32, 16])
    s5_dsk = din("s5_dsk", [128, D])
    lru_w_in = din("lru_w_in", [D, 2 * D])
    lru_w_out = din("lru_w_out", [D, D])
    lru_conv = din("lru_conv", [128, 5, KT])
    lru_gw = din("lru_gw", [128, 4, KT, 128])
    lru_gp = din("lru_gp", [128, 2, 3, KT])

    y_out = dout("y_out", [D, T])
    st5_out = dout("st5_out", [128, 8, 8, 2, 8])
    stl_out = dout("stl_out", [128, 2, KT, 8])

    ctx = []

    def sb(name, shape, dt=F32):
        t = nc.sbuf_tensor(name, list(shape), dt)
        ctx.append(t)
        return t.__enter__()

    def ps(name):
        t = nc.psum_tensor(name, [128, 512], F32)
        ctx.append(t)
        return t.__enter__()

    xs = sb("xs", [128, KT, T])
    hm = sb("hm", [128, KT, T], BF16)
    SCRN = 29900
    scr = sb("scr", [128, SCRN])
    cst = sb("cst", [128, 128 * 3])
    identb = sb("identb", [128, 128], BF16)
    Jb = sb("Jb", [128, 128], BF16)
    onesb = sb("onesb", [128, 128], BF16)
    mkF = sb("mkF", [128, 2, 256], BF16)
    mkB = sb("mkB", [128, 2, 256], BF16)
    modt = sb("modt", [128, 2, 72])
    adab = sb("adab", [128, 2, 72])
    lng = sb("lng", [128, 2, 3, KT])
    lnb = sb("lnb", [128, 2, 3, KT])
    sm = sb("sm", [128, 320])
    condt = sb("condt", [128, KT])
    condb = sb("condb", [128, KT], BF16)
    flags = sb("flags", [128, 4])
    maskdc = sb("maskdc", [128, 256])
    banks = [ps("ps%d" % i) for i in range(8)]
    dummy = sb("dummyt", [128, 8])
    identf = cst[:, 0:128]
    Jf = cst[:, 256:384]

    def carve(off, shape, dt=F32):
        n = int(np.prod(shape))
        words = n if dt == F32 else (n + 1) // 2
        v = scr[:, off:off + words]
        if dt != F32:
            v = v.bitcast(dt)
        if len(shape) > 1:
            names = " ".join("a%d" % i for i in range(len(shape)))
            kw = {"a%d" % i: shape[i] for i in range(1, len(shape))}
            v = v.rearrange("p (%s) -> p %s" % (names, names), **kw)
        return v, off + words

    sm_off = [0]

    def smalloc(n):
        o = sm_off[0]
        sm_off[0] += n
        assert sm_off[0] <= 320
        return sm[:, o:o + n]

    def dma(out, in_, w, r=(), stream=None, slow=False):
        if slow:
            P.add('sp', lambda e, out=out, in_=in_: e.dma_start(out=out, in_=in_, allow_slow_non_contiguous=True), r=r, w=w, stream=stream)
        else:
            P.add('sp', lambda e, out=out, in_=in_: e.dma_start(out=out, in_=in_), r=r, w=w, stream=stream)

    dma(cst[:], consts_in[:, 0:384], [('cst',)], stream='cst')
    mskst, _ = carve(20000, [1024])
    dma(mskst, consts_in[:, 384:1408], [('mskst',)], stream='mskst')
    dma(condt[:], cond_in[:, :], [('cond',)], stream='cond')
    dma(flags[:], flags_in[:, :], [('flags',)], stream='flags')
    dma(maskdc[:], maskdc_in[:, :], [('maskdc',)], stream='maskdc')
    dma(adab[:], ada_b[:, :, :], [('adab',)], stream='adab')
    dma(lng[:], ln_g[:, :, :, :], [('lng',)], stream='lng')
    dma(lnb[:], ln_b[:, :, :, :], [('lnb',)], stream='lnb')
    for kt in range(KT):
        dma(xs[:, kt, :], x_in[kt * 128:(kt + 1) * 128, :], [('xs', kt, b) for b in range(NBLK)], stream='x%d' % (kt % 2))

    P.add('dve', lambda e: e.tensor_copy(out=identb[:], in_=cst[:, 0:128]), r=[('cst',)], w=[('identb',)])
    P.add('dve', lambda e: e.tensor_copy(out=onesb[:], in_=cst[:, 128:256]), r=[('cst',)], w=[('onesb',)])
    P.add('dve', lambda e: e.tensor_copy(out=Jb[:], in_=cst[:, 256:384]), r=[('cst',)], w=[('Jb',)])
    P.add('dve', lambda e: e.tensor_copy(out=mkF[:], in_=mskst[:, 0:512].rearrange("p (a b) -> p a b", a=2)), r=[('mskst',)], w=[('mkF',)])
    P.add('dve', lambda e: e.tensor_copy(out=mkB[:], in_=mskst[:, 512:1024].rearrange("p (a b) -> p a b", a=2)), r=[('mskst',)], w=[('mkB',)])

    pe_w, o_pe = carve(14000, [2, 4])
    Es, o_pe = carve(o_pe, [2, 64])
    Ec, o_pe = carve(o_pe, [2, 64])
    pidx, o_pe = carve(o_pe, [2])
    ptmp, o_pe = carve(o_pe, [2, 4])
    pidf, o_pe = carve(o_pe, [2])
    P.add('pool', lambda e: e.iota(pidf, pattern=[[128, 2]], base=0, channel_multiplier=1, allow_small_or_imprecise_dtypes=True),
          w=[('pidf',)])
    P.add('act', lambda e: e.activation(out=pe_w[:, :, 0], in_=pidf, func=AF.Exp, scale=-math.log(10000.0) / 256.0),
          r=[('pidf',)], w=[('pe_w', 0)])
    P.add('act', lambda e: e.activation(out=pe_w[:, :, 1], in_=pe_w[:, :, 0], func=AF.Sin), r=[('pe_w', 0)], w=[('pe_w', 1)])
    P.add('act', lambda e: e.activation(out=pe_w[:, :, 3], in_=pe_w[:, :, 0], func=AF.Sin, scale=0.5), r=[('pe_w', 0)], w=[('pe_w', 3)])
    P.add('dve', lambda e: e.tensor_tensor(out=pe_w[:, :, 2], in0=pe_w[:, :, 3], in1=pe_w[:, :, 3], op=ALU.mult), r=[('pe_w', 3)], w=[('pe_w', 2)])
    P.add('dve', lambda e: e.tensor_scalar(out=pe_w[:, :, 2], in0=pe_w[:, :, 2], scalar1=-2.0, scalar2=1.0, op0=ALU.mult, op1=ALU.add),
          r=[('pe_w', 2)], w=[('pe_w', 2)])
    P.add('dve', lambda e: e.memset(Es[:, :, 0:1], 0.0), w=[('Es',)])
    P.add('dve', lambda e: e.memset(Ec[:, :, 0:1], 1.0), w=[('Ec',)])
    for n in range(63):
        def stepfn(e, n=n):
            e.tensor_tensor(out=ptmp[:, :, 0], in0=Es[:, :, n], in1=pe_w[:, :, 2], op=ALU.mult)
            e.tensor_tensor(out=ptmp[:, :, 1], in0=Ec[:, :, n], in1=pe_w[:, :, 1], op=ALU.mult)
            e.tensor_tensor(out=ptmp[:, :, 2], in0=Ec[:, :, n], in1=pe_w[:, :, 2], op=ALU.mult)
            e.tensor_tensor(out=ptmp[:, :, 3], in0=Es[:, :, n], in1=pe_w[:, :, 1], op=ALU.mult)
            e.tensor_tensor(out=Es[:, :, n + 1], in0=ptmp[:, :, 0], in1=ptmp[:, :, 1], op=ALU.add)
            return e.tensor_tensor(out=Ec[:, :, n + 1], in0=ptmp[:, :, 2], in1=ptmp[:, :, 3], op=ALU.subtract)
        P.seq('dve', stepfn, r=[('pe_w', 1), ('pe_w', 2), ('Es',), ('Ec',)], w=[('Es',), ('Ec',), ('ptmp',)])
    P.add('dve', lambda e: e.tensor_scalar(out=Es[:], in0=Es[:], scalar1=flags[:, 0:1], scalar2=ALPHA, op0=ALU.mult, op1=ALU.mult),
          r=[('Es',), ('flags',)], w=[('Es',)])
    P.add('dve', lambda e: e.tensor_scalar(out=Ec[:], in0=Ec[:], scalar1=flags[:, 0:1], scalar2=ALPHA, op0=ALU.mult, op1=ALU.mult),
          r=[('Ec',), ('flags',)], w=[('Ec',)])
    for kt in range(KT):
        par = kt % 2
        q = kt // 2
        tab = Es if q % 2 == 0 else Ec
        xv = xs[:, kt, :].rearrange("p (r c) -> p r c", c=64)
        if q < 2:
            ev = tab[:, par, 0:32].unsqueeze(2).broadcast_to([128, 32, 64])
        else:
            ev = tab[:, par, 0:64].unsqueeze(1).broadcast_to([128, 32, 64])
        P.add('dve', lambda e, xv=xv, ev=ev: e.scalar_tensor_tensor(out=xv, in0=xv, scalar=ALPHA, in1=ev, op0=ALU.mult, op1=ALU.add),
              r=[('Es',), ('Ec',)] + [('xs', kt, b) for b in range(NBLK)], w=[('xs', kt, b) for b in range(NBLK)])
    P.add('act', lambda e: e.activation(out=condb[:], in_=condt[:], func=AF.Silu), r=[('cond',)], w=[('condb',)])
    ada_cnt = [0]

    def ada_bufs(width, off):
        ast, o_ = carve(off, [2, KT, width])
        abf, o_ = carve(o_, [2, KT, width], BF16)
        rwt, o_ = carve(o_, [2, width])
        return ast, abf, rwt

    def ada_load(layer, ch, width, off, tag):
        ast, abf, rwt = ada_bufs(width, off)
        sl = ch % 2
        src = ada_w[layer, :, ch * width:(ch + 1) * width].rearrange("(k p) c -> p k c", p=128)
        dma(ast[:, sl], src, [(tag + 'st', sl)], stream=tag + 'st%d' % sl)
        if sl == 0:
            P.add('act', lambda e: e.copy(out=abf[:, sl], in_=ast[:, sl]), r=[(tag + 'st', sl)], w=[(tag + 'bf', sl)])
        else:
            P.add('dve', lambda e: e.tensor_copy(out=abf[:, sl], in_=ast[:, sl]), r=[(tag + 'st', sl)], w=[(tag + 'bf', sl)])

    def ada_compute(layer, ch, width, off, tag):
        ast, abf, rwt = ada_bufs(width, off)
        sl = ch % 2
        nmt = width // 128
        for kt in range(KT):
            P.add('pe', lambda e, kt=kt: e.matmul(banks[6][0:1, 0:width], condb[:, kt:kt + 1], abf[:, sl, kt, :],
                                                   start=(kt == 0), stop=(kt == KT - 1)),
                  r=[(tag + 'bf', sl), ('condb',)], w=[('ps', 6)])
        P.add('act', lambda e: e.copy(out=rwt[0:1, sl, :], in_=banks[6][0:1, 0:width]), r=[('ps', 6)], w=[(tag + 'row', sl)])
        m0 = ch * nmt
        for mm in range(nmt):
            P.add('pe', lambda e, mm=mm: e.matmul(banks[7][:, mm:mm + 1], rwt[0:1, sl, mm * 128:(mm + 1) * 128], cst[0:1, 128:129],
                                                   start=True, stop=True), r=[(tag + 'row', sl), ('cst',)], w=[('ps', 7)])
        P.add('dve', lambda e: e.tensor_tensor(out=modt[:, layer, m0:m0 + nmt], in0=banks[7][:, 0:nmt], in1=adab[:, layer, m0:m0 + nmt], op=ALU.add),
              r=[('ps', 7), ('adab',)], w=[('modt', layer)])

    def ada_chunk(layer, ch, width, off, tag):
        ada_load(layer, ch, width, off, tag)
        ada_compute(layer, ch, width, off, tag)

    for ch in range(18):
        ada_chunk(0, ch, 512, 0, 'adaA')

    def modv(layer, sub, which):
        o = (sub * 3 + which) * 8
        return modt[:, layer, o:o + 8]

    s1 = {}
    shv = {}
    gtv = {}
    for layer in range(2):
        for sub in range(3):
            s1[(layer, sub)] = smalloc(8)
            shv[(layer, sub)] = modv(layer, sub, 0)
            gtv[(layer, sub)] = smalloc(8)

    def derive(layer):
        for sub in range(3):
            a = s1[(layer, sub)]
            P.add('dve', lambda e, a=a, layer=layer, sub=sub: e.tensor_scalar(
                out=a, in0=modv(layer, sub, 1), scalar1=1.0, scalar2=1.0 / ALPHA, op0=ALU.add, op1=ALU.mult),
                r=[('modt', layer)], w=[('sm', 's1', layer, sub)])
            g = gtv[(layer, sub)]
            wgt = 1.0 if sub == 1 else 0.5
            P.add('dve', lambda e, g=g, layer=layer, sub=sub, wgt=wgt: e.tensor_scalar(
                out=g, in0=modv(layer, sub, 2), scalar1=wgt, scalar2=None, op0=ALU.mult),
                r=[('modt', layer)], w=[('sm', 'gt', layer, sub)])
    derive(0)
    lnG = {}
    lnB = {}
    hS = {}
    hB = {}
    order = [(l, s) for l in range(2) for s in range(3)]
    for idx, (layer, sub) in enumerate(order):
        final = (idx == len(order) - 1)
        sc = 1.0 if final else ALPHA
        gg = smalloc(8)
        bb = smalloc(8)
        P.add('dve', lambda e, gg=gg, layer=layer, sub=sub, sc=sc: e.tensor_scalar(
            out=gg, in0=lng[:, layer, sub, :], scalar1=sc, scalar2=None, op0=ALU.mult), r=[('lng',)], w=[('sm', 'lnG', idx)])
        P.add('dve', lambda e, bb=bb, layer=layer, sub=sub, sc=sc: e.tensor_scalar(
            out=bb, in0=lnb[:, layer, sub, :], scalar1=sc, scalar2=None, op0=ALU.mult), r=[('lnb',)], w=[('sm', 'lnB', idx)])
        lnG[idx] = gg
        lnB[idx] = bb
        if not final:
            nl, nsub = order[idx + 1]
            hB[idx] = (nl, nsub)

    for kt in range(KT):
        for b in range(NBLK):
            P.add('act', lambda e, kt=kt, b=b: e.activation(
                out=hm[:, kt, b * BLK:(b + 1) * BLK], in_=xs[:, kt, b * BLK:(b + 1) * BLK], func=AF.Identity,
                scale=s1[(0, 0)][:, kt:kt + 1], bias=shv[(0, 0)][:, kt:kt + 1]),
                r=[('xs', kt, b), ('sm', 's1', 0, 0), ('modt', 0)], w=[('hm', kt, b)])
    scale_ops = []
    for kt in range(KT):
        for b in range(NBLK):
            scale_ops.append(dict(eng='pool', fn=(lambda e, kt=kt, b=b: e.tensor_scalar(
                out=xs[:, kt, b * BLK:(b + 1) * BLK], in0=xs[:, kt, b * BLK:(b + 1) * BLK], scalar1=ALPHA, scalar2=None,
                op0=ALU.mult)), r=[('xs', kt, b)], w=[('xs', kt, b)], stream=None))
    nhm = KT * NBLK

    LNB = 3600

    def ffn(layer, sub, ln_idx, soft=False, hook=None):
        j = sub // 2
        wu = w_up[layer, j]
        wd = w_down[layer, j]
        o = LNB
        hbuf, o = carve(o, [6, T], BF16)
        ust, o = carve(o, [2, KT, 128])
        ubf, o = carve(o, [4, KT, 128], BF16)
        dst_, o = carve(o, [2, 6, 128])
        dbf, o = carve(o, [2, 6, 128], BF16)
        sg, o = carve(o, [2, BLK])
        dall, o = carve(o, [KT, 5, 128], BF16)
        lo = 0
        gt = gtv[(layer, sub)]
        uci = [0]

        def load_up(f, ag):
            sl = uci[0] % 2
            uci[0] += 1
            col = ag * DFF + f * 128
            src = wu[:, col:col + 128].rearrange("(k p) c -> p k c", p=128)
            dma(ust[:, sl], src, [('ust', sl)], stream='ust%d' % sl)
            bs = (f % 2) * 2 + ag
            P.add('pool', lambda e, sl=sl, bs=bs: e.tensor_copy(out=ubf[:, bs], in_=ust[:, sl]),
                  r=[('ust', sl)], w=[('ubf', bs)])

        dci = [0]

        def load_down(f0, nf, m):
            sl = dci[0] % 2
            dci[0] += 1
            src = wd[f0 * 128:(f0 + nf) * 128, m * 128:(m + 1) * 128].rearrange("(f p) c -> p f c", p=128)
            dma(dst_[:, sl, 0:nf], src, [('dst', sl)], stream='dst%d' % sl)
            P.add('pool', lambda e, sl=sl, nf=nf: e.tensor_copy(out=dbf[:, sl, 0:nf], in_=dst_[:, sl, 0:nf]),
                  r=[('dst', sl)], w=[('dbf', sl)])
            return sl

        pcount = [0]
        for (f0, f1) in SECTIONS:
            nf = f1 - f0
            load_up(f0, 0)
            load_up(f0, 1)
            for f in range(f0, f1):
                if f + 1 < f1:
                    load_up(f + 1, 0)
                    load_up(f + 1, 1)
                for b in range(NBLK):
                    pp = pcount[0] % 2
                    pcount[0] += 1
                    pa = banks[pp * 2]
                    pg = banks[pp * 2 + 1]
                    for ag, pt in ((0, pa), (1, pg)):
                        bs = (f % 2) * 2 + ag
                        for kt in range(KT):
                            P.add('pe', lambda e, pt=pt, bs=bs, kt=kt, b=b: e.matmul(
                                pt[:, :], ubf[:, bs, kt, :], hm[:, kt, b * BLK:(b + 1) * BLK],
                                start=(kt == 0), stop=(kt == KT - 1)),
                                r=[('ubf', bs), ('hm', kt, b)], w=[('ps', pp * 2 + ag)])
                    P.add('act', lambda e, pg=pg, pp=pp: e.activation(out=sg[:, pp], in_=pg[:, :], func=AF.Silu),
                          r=[('ps', pp * 2 + 1)], w=[('sg', pp)])
                    P.add('dve', lambda e, pa=pa, pp=pp, f=f, f0=f0, b=b: e.tensor_tensor(
                        out=hbuf[:, f - f0, b * BLK:(b + 1) * BLK], in0=sg[:, pp], in1=pa[:, :], op=ALU.mult),
                        r=[('sg', pp), ('ps', pp * 2)], w=[('h', f - f0, b)])
                if hook is not None:
                    hook(f)
            last = (f1 == NF)
            if not last:
                sl_next = load_down(f0, nf, 0)
                for m in range(KT):
                    sl = sl_next
                    if m + 1 < KT:
                        sl_next = load_down(f0, nf, m + 1)
                    for b in range(NBLK):
                        pb = 4 + (pcount[0] % 2)
                        pcount[0] += 1
                        for fi in range(nf):
                            P.add('pe', lambda e, pb=pb, sl=sl, fi=fi, b=b, nf=nf: e.matmul(
                                banks[pb][:, :], dbf[:, sl, fi, :], hbuf[:, fi, b * BLK:(b + 1) * BLK],
                                start=(fi == 0), stop=(fi == nf - 1)),
                                r=[('dbf', sl), ('h', fi, b)], w=[('ps', pb)])
                        P.add('dve', lambda e, pb=pb, m=m, b=b: e.scalar_tensor_tensor(
                            out=xs[:, m, b * BLK:(b + 1) * BLK], in0=banks[pb][:, :], scalar=gt[:, m:m + 1],
                            in1=xs[:, m, b * BLK:(b + 1) * BLK], op0=ALU.mult, op1=ALU.add),
                            r=[('ps', pb), ('xs', m, b), ('sm', 'gt', layer, sub)], w=[('xs', m, b)])
            else:
                for m in range(KT):
                    sl = dci[0] % 2
                    dci[0] += 1
                    src = wd[f0 * 128:(f0 + nf) * 128, m * 128:(m + 1) * 128].rearrange("(f p) c -> p f c", p=128)
                    dma(dst_[:, sl, 0:nf], src, [('dst', sl)], stream='dst%d' % sl)
                    P.add('pool', lambda e, sl=sl, nf=nf, m=m: e.tensor_copy(out=dall[:, m, 0:nf], in_=dst_[:, sl, 0:nf]),
                          r=[('dst', sl)], w=[('dall', m)])
                for b in range(NBLK + 1):
                    if b < NBLK:
                        for m in range(KT):
                            pb = 4 + (pcount[0] % 2)
                            pcount[0] += 1
                            for fi in range(nf):
                                P.add('pe', lambda e, pb=pb, m=m, fi=fi, b=b, nf=nf: e.matmul(
                                    banks[pb][:, :], dall[:, m, fi, :], hbuf[:, fi, b * BLK:(b + 1) * BLK],
                                    start=(fi == 0), stop=(fi == nf - 1)),
                                    r=[('dall', m), ('h', fi, b)], w=[('ps', pb)])
                            P.add('dve', lambda e, pb=pb, m=m, b=b: e.scalar_tensor_tensor(
                                out=xs[:, m, b * BLK:(b + 1) * BLK], in0=banks[pb][:, :], scalar=gt[:, m:m + 1],
                                in1=xs[:, m, b * BLK:(b + 1) * BLK], op0=ALU.mult, op1=ALU.add),
                                r=[('ps', pb), ('xs', m, b), ('sm', 'gt', layer, sub)], w=[('xs', m, b)])
                    if soft and b == NBLK - 1:
                        P.barrier(dummy[:])
                    if b >= 1:
                        layernorm(b - 1, ln_idx, lo)

    def layernorm(b, idx, lo):
        final = (idx == 5)
        o = lo
        zb, o = carve(o, [2, BLK], BF16)
        zq, o = carve(o, [2, BLK], BF16)
        mean, o = carve(o, [BLK])
        var, o = carve(o, [BLK])
        rstd, o = carve(o, [BLK])
        tt, o = carve(o, [2, BLK])
        cs = slice(b * BLK, (b + 1) * BLK)
        for kt in range(KT):
            sl = kt % 2
            P.add('act', lambda e, kt=kt, sl=sl: e.copy(out=zb[:, sl], in_=xs[:, kt, cs]),
                  r=[('xs', kt, b)], w=[('zb', sl)])
            P.add('act', lambda e, kt=kt, sl=sl: e.activation(out=zq[:, sl], in_=xs[:, kt, cs], func=AF.Square),
                  r=[('xs', kt, b)], w=[('zq', sl)])
            P.add('pe', lambda e, kt=kt, sl=sl: e.matmul(banks[6][:, :], onesb[:, :], zb[:, sl], start=(kt == 0), stop=(kt == KT - 1)),
                  r=[('zb', sl), ('onesb',)], w=[('ps', 6)])
            P.add('pe', lambda e, kt=kt, sl=sl: e.matmul(banks[7][:, :], onesb[:, :], zq[:, sl], start=(kt == 0), stop=(kt == KT - 1)),
                  r=[('zq', sl), ('onesb',)], w=[('ps', 7)])
        P.add('dve', lambda e: e.tensor_scalar(out=mean, in0=banks[6][:, :], scalar1=1.0 / D, scalar2=None, op0=ALU.mult),
              r=[('ps', 6)], w=[('mean',)])
        P.add('dve', lambda e: e.tensor_tensor(out=var, in0=mean, in1=mean, op=ALU.mult), r=[('mean',)], w=[('var',)])
        P.add('dve', lambda e: e.scalar_tensor_tensor(out=var, in0=banks[7][:, :], scalar=1.0 / D, in1=var,
                                                      op0=ALU.mult, op1=ALU.subtract),
              r=[('ps', 7), ('var',)], w=[('var',)])
        P.add('dve', lambda e: e.tensor_scalar(out=var, in0=var, scalar1=LN_EPS, scalar2=None, op0=ALU.add), r=[('var',)], w=[('var',)])
        P.add('act', lambda e: e.activation(out=rstd, in_=var, func=AF.Sqrt), r=[('var',)], w=[('rstd',)])
        P.add('dve', lambda e: e.reciprocal(out=rstd, in_=rstd), r=[('rstd',)], w=[('rstd',)])
        for kt in range(KT):
            sl = kt % 2
            P.add('dve', lambda e, kt=kt, sl=sl: e.tensor_tensor(out=tt[:, sl], in0=xs[:, kt, cs], in1=mean, op=ALU.subtract),
                  r=[('xs', kt, b), ('mean',)], w=[('tt', sl)])
            P.add('dve', lambda e, kt=kt, sl=sl: e.scalar_tensor_tensor(
                out=tt[:, sl], in0=tt[:, sl], scalar=lnG[idx][:, kt:kt + 1], in1=rstd, op0=ALU.mult, op1=ALU.mult),
                r=[('tt', sl), ('rstd',), ('sm', 'lnG', idx)], w=[('tt', sl)])
            P.add('act', lambda e, kt=kt, sl=sl: e.activation(
                out=xs[:, kt, cs], in_=tt[:, sl], func=AF.Identity, bias=lnB[idx][:, kt:kt + 1], scale=1.0),
                r=[('tt', sl), ('sm', 'lnB', idx)], w=[('xs', kt, b)])
            if final:
                dma(y_out[kt * 128:(kt + 1) * 128, cs], xs[:, kt, cs], [('yout', kt, b)], r=[('xs', kt, b)], stream='yo%d' % (kt % 4))
            else:
                nl, nsub = hB[idx]
                P.add('act', lambda e, kt=kt, nl=nl, nsub=nsub: e.activation(
                    out=hm[:, kt, cs], in_=xs[:, kt, cs], func=AF.Identity,
                    scale=s1[(nl, nsub)][:, kt:kt + 1], bias=shv[(nl, nsub)][:, kt:kt + 1]),
                    r=[('xs', kt, b), ('sm', 's1', nl, nsub), ('modt', nl)], w=[('hm', kt, b)])


    class Loader:
        def __init__(self, name, o, nst=2, nbf=4):
            self.name = name
            self.st, o = carve(o, [nst, KT, 128])
            self.bf, o = carve(o, [nbf, KT, 128], BF16)
            self.nst, self.nbf, self.i, self.end = nst, nbf, 0, o

        def load(self, src):
            sl = self.i % self.nst
            bs = self.i % self.nbf
            self.i += 1
            dma(self.st[:, sl], src, [(self.name + 'st', sl)], stream=self.name + 'st%d' % sl)
            P.add('pool', lambda e, sl=sl, bs=bs: e.tensor_copy(out=self.bf[:, bs], in_=self.st[:, sl]),
                  r=[(self.name + 'st', sl)], w=[(self.name + 'bf', bs)])
            return bs

    def wsrc(wap, col):
        return wap[:, col:col + 128].rearrange("(k p) c -> p k c", p=128)

    def s5_mixer():
        layer, sub, ln_idx = 0, 1, 1
        o = 0
        yfm, o = carve(o, [KT, T], BF16)
        ld = Loader('s5w', o, 1, 2)
        o = ld.end
        par, o = carve(o, [2, 32, 3])
        s5i, o = carve(o, [2, 32, 2])
        dskt, o = carve(o, [128])
        utm, o = carve(o, [8, 16, 16])
        U, o = carve(o, [2, 8, 128], BF16)
        SL, o = carve(o, [2, 4, 256])
        SprevZ, o = carve(o, [2, 4, 2, 2, 128], BF16)
        Sprev = SprevZ[:].rearrange("p a b c d e -> p (a b c d e)")[:, 0:2048].rearrange("p (a b c d) -> p a b c d", a=2, b=4, c=2)
        mtb, o = carve(o, [512])
        t1p, o = carve(o, [2, 256])
        Kall, o = carve(o, [2, 2, 2, 256], BF16)
        Cma, o = carve(o, [2, 4, 2, 256], BF16)
        Gp, o = carve(o, [2, 2, 256], BF16)
        Hp, o = carve(o, [2, 2, 256], BF16)
        Fm, o = carve(o, [2, 2, 2, 64], BF16)
        t1, o = carve(o, [2, 256])
        Pa2, o = carve(o, [2, 2, 2, 4, 32])
        Pd2, o = carve(o, [2, 2, 2, 4, 32])
        tb, o = carve(o, [15, 2, 32])
        bbp2, o = carve(o, [2, 2, 2, 4, 16])
        bcp2, o = carve(o, [2, 4, 4, 16])
        scn, o = carve(o, [3, 8, 2, 8])
        scn2, o = carve(o, [3, 8, 2])
        Wt, o = carve(o, [2, 2, 4, 16])
        ptw, o = carve(o, [2, 2, 4, 2])
        wtmp, o = carve(o, [2, 4, 8])
        ArX, o = carve(o, [8, 2])
        AiX, o = carve(o, [8, 2])
        A2rX, o = carve(o, [8, 2])
        A2iX, o = carve(o, [8, 2])
        Bnd, o = carve(o, [9, 8, 2])
        CinB, o = carve(o, [8, 2, 8])
        ytm = Sprev[:].rearrange("p a b c d -> p (a b c d)").rearrange("p (t c) -> p t c", t=16)
        ptw2, o = carve(o, [2, 2, 4, 8])
        LNt, o = carve(o, [2, 2, 2, 4])
        assert o <= SCRN, o
        dma(par[:], s5_par[:, :, :, :], [('s5par',)], stream='s5par')
        dma(s5i[:], s5init_in[:, :, :, :], [('s5i',)], stream='s5i')
        TB = lambda n: tb[:, n]
        K = [('s5tb',)]

        def pool(fn, r=K, w=K):
            P.seq('pool', fn, r=r, w=w)

        def tt(e, out, a, b, op):
            return e.tensor_tensor(out=out, in0=a, in1=b, op=op)

        are, aim, ldt = par[:, :, :, 0], par[:, :, :, 1], par[:, :, :, 2]
        P.add('act', lambda e: e.activation(out=TB(0), in_=ldt, func=AF.Exp), r=[('s5par',)], w=K)
        pool(lambda e: tt(e, TB(1), TB(0), are, ALU.mult), r=K + [('s5par',)])
        pool(lambda e: tt(e, TB(2), TB(0), aim, ALU.mult))
        P.add('act', lambda e: e.activation(out=TB(3), in_=TB(1), func=AF.Exp, scale=0.125), r=K, w=K)
        P.add('act', lambda e: e.activation(out=TB(4), in_=TB(2), func=AF.Sin, scale=0.125), r=K, w=K)
        P.add('act', lambda e: e.activation(out=TB(5), in_=TB(2), func=AF.Sin, scale=0.0625), r=K, w=K)
        pool(lambda e: tt(e, TB(5), TB(5), TB(5), ALU.mult))
        pool(lambda e: e.tensor_scalar(out=TB(5), in0=TB(5), scalar1=-2.0, scalar2=1.0, op0=ALU.mult, op1=ALU.add))
        pool(lambda e: tt(e, TB(6), TB(3), TB(5), ALU.mult))
        pool(lambda e: tt(e, TB(7), TB(3), TB(4), ALU.mult))
        for _ in range(3):
            def sq(e):
                tt(e, TB(8), TB(6), TB(6), ALU.mult)
                tt(e, TB(9), TB(7), TB(7), ALU.mult)
                tt(e, TB(10), TB(6), TB(7), ALU.mult)
                tt(e, TB(6), TB(8), TB(9), ALU.subtract)
                return e.tensor_scalar(out=TB(7), in0=TB(10), scalar1=2.0, scalar2=None, op0=ALU.mult)
            pool(sq)
        def qfn(e):
            tt(e, TB(8), are, are, ALU.mult)
            tt(e, TB(9), aim, aim, ALU.mult)
            tt(e, TB(8), TB(8), TB(9), ALU.add)
            e.tensor_scalar(out=TB(9), in0=TB(6), scalar1=-1.0, scalar2=None, op0=ALU.add)
            tt(e, TB(10), TB(9), are, ALU.mult)
            tt(e, TB(11), TB(7), aim, ALU.mult)
            tt(e, TB(10), TB(10), TB(11), ALU.add)
            tt(e, TB(11), TB(7), are, ALU.mult)
            tt(e, TB(12), TB(9), aim, ALU.mult)
            return tt(e, TB(11), TB(11), TB(12), ALU.subtract)
        pool(qfn, r=K + [('s5par',)])
        P.add('dve', lambda e: e.reciprocal(out=TB(8), in_=TB(8)), r=K, w=K)
        pool(lambda e: tt(e, TB(10), TB(10), TB(8), ALU.mult))
        pool(lambda e: tt(e, TB(11), TB(11), TB(8), ALU.mult))
        def invfn(e):
            tt(e, TB(12), TB(6), TB(6), ALU.mult)
            tt(e, TB(13), TB(7), TB(7), ALU.mult)
            return tt(e, TB(12), TB(12), TB(13), ALU.add)
        pool(invfn)
        P.add('dve', lambda e: e.reciprocal(out=TB(12), in_=TB(12)), r=K, w=K)
        pool(lambda e: tt(e, TB(13), TB(6), TB(12), ALU.mult))
        pool(lambda e: e.scalar_tensor_tensor(out=TB(14), in0=TB(7), scalar=-1.0, in1=TB(12), op0=ALU.mult, op1=ALU.mult)
             if False else tt(e, TB(14), TB(7), TB(12), ALU.mult))
        pool(lambda e: e.tensor_scalar(out=TB(14), in0=TB(14), scalar1=-1.0, scalar2=None, op0=ALU.mult))

        bidx = 0
        for eg in range(8):
            g2s = slice(eg * 4, eg * 4 + 4)
            par_ = eg % 2
            KE = [('s5e', par_)]
            Pa, Pd, bbp, bcp = Pa2[:, par_], Pd2[:, par_], bbp2[:, par_], bcp2[:, par_]
            dma(bcp, s5_bc[:, :, g2s, :], [('s5bc', par_)], stream='s5bc%d' % par_)
            bs = ld.load(wsrc(s5_w_in, eg * 128))
            dma(dskt, s5_dsk[:, eg * 128:(eg + 1) * 128], [('dskt',)], stream='dskt')
            def cmulb(e, o_re, o_im, i_re, i_im, s_re, s_im, n):
                m1, m2 = ptw2[:, 0, :, :, 0:n], ptw2[:, 1, :, :, 0:n]
                bcn = lambda ap_: ap_.unsqueeze(3).broadcast_to([128, 2, 4, n])
                tt(e, m1, i_re, bcn(s_re), ALU.mult)
                tt(e, m2, i_im, bcn(s_im), ALU.mult)
                tt(e, o_re, m1, m2, ALU.subtract)
                tt(e, m1, i_re, bcn(s_im), ALU.mult)
                tt(e, m2, i_im, bcn(s_re), ALU.mult)
                tt(e, o_im, m1, m2, ALU.add)

            def powfn(e, g2s=g2s, Pa=Pa, Pd=Pd, bbp=bbp, bcp=bcp):
                e.tensor_copy(out=LNt[:, 0, 0], in_=TB(6)[:, :, g2s])
                e.tensor_copy(out=LNt[:, 1, 0], in_=TB(7)[:, :, g2s])
                e.tensor_copy(out=LNt[:, 0, 1], in_=TB(13)[:, :, g2s])
                e.tensor_copy(out=LNt[:, 1, 1], in_=TB(14)[:, :, g2s])
                for tab, c0 in ((Pa, 15), (Pd, 16)):
                    e.memset(tab[:, 0, :, :, c0], 1.0)
                    e.memset(tab[:, 1, :, :, c0], 0.0)
                w = ptw
                for n in (1, 2, 4, 8):
                    L = (LNt[:, 0, 0], LNt[:, 1, 0])
                    Li = (LNt[:, 0, 1], LNt[:, 1, 1])
                    cmulb(e, Pa[:, 0, :, :, 15 + n:15 + 2 * n], Pa[:, 1, :, :, 15 + n:15 + 2 * n], Pa[:, 0, :, :, 15:15 + n], Pa[:, 1, :, :, 15:15 + n], L[0], L[1], n)
                    cmulb(e, Pa[:, 0, :, :, 16 - 2 * n:16 - n], Pa[:, 1, :, :, 16 - 2 * n:16 - n], Pa[:, 0, :, :, 16 - n:16], Pa[:, 1, :, :, 16 - n:16], Li[0], Li[1], n)
                    cmulb(e, Pd[:, 0, :, :, 17 - 2 * n:17 - n], Pd[:, 1, :, :, 17 - 2 * n:17 - n], Pd[:, 0, :, :, 17 - n:17], Pd[:, 1, :, :, 17 - n:17], L[0], L[1], n)
                    cmulb(e, Pd[:, 0, :, :, 16 + n:16 + 2 * n], Pd[:, 1, :, :, 16 + n:16 + 2 * n], Pd[:, 0, :, :, 16:16 + n], Pd[:, 1, :, :, 16:16 + n], Li[0], Li[1], n)
                    for k in range(2):
                        xr, xi = LNt[:, 0, k], LNt[:, 1, k]
                        a0, a1, a2 = [w[:, i // 2, i % 2].rearrange("p a b -> p b a") for i in range(3)]
                        tt(e, a0, xr, xr, ALU.mult)
                        tt(e, a1, xi, xi, ALU.mult)
                        tt(e, a2, xr, xi, ALU.mult)
                        tt(e, xr, a0, a1, ALU.subtract)
                        e.tensor_scalar(out=xi, in0=a2, scalar1=2.0, scalar2=None, op0=ALU.mult)
                for ri in range(2):
                    e.tensor_copy(out=Pa[:, ri, :, :, 31], in_=LNt[:, ri, 0])
                    e.tensor_copy(out=Pd[:, ri, :, :, 0], in_=LNt[:, ri, 0])
                qr = TB(10)[:, :, g2s].unsqueeze(3).broadcast_to([128, 2, 4, 16])
                qi = TB(11)[:, :, g2s].unsqueeze(3).broadcast_to([128, 2, 4, 16])
                br = bcp[:, 0, :, :].unsqueeze(1).broadcast_to([128, 2, 4, 16])
                bi = bcp[:, 1, :, :].unsqueeze(1).broadcast_to([128, 2, 4, 16])
                x0 = ptw2[:, 0]
                x1 = ptw2[:, 1]
                for half in range(2):
                    hs = slice(half * 8, half * 8 + 8)
                    tt(e, x0, qr[:, :, :, hs], br[:, :, :, hs], ALU.mult)
                    tt(e, x1, qi[:, :, :, hs], bi[:, :, :, hs], ALU.mult)
                    tt(e, bbp[:, 0, :, :, hs], x0, x1, ALU.subtract)
                    tt(e, x0, qr[:, :, :, hs], bi[:, :, :, hs], ALU.mult)
                    tt(e, x1, qi[:, :, :, hs], br[:, :, :, hs], ALU.mult)
                    tt(e, bbp[:, 1, :, :, hs], x0, x1, ALU.add)
            P.seq('pool', powfn, r=K + [('s5bc', par_)], w=KE + [('ptw',)])
            for k4 in range(4):
                bk = banks[4 + (k4 % 2)]
                for kk in range(4):
                    k = k4 * 4 + kk
                    for kt in range(KT):
                        P.add('pe', lambda e, bk=bk, kk=kk, k=k, kt=kt, bs=bs: e.matmul(
                            bk[:, kk * 128:(kk + 1) * 128], hm[:, kt, k::16], ld.bf[:, bs, kt, :],
                            start=(kt == 0), stop=(kt == KT - 1)),
                            r=[('s5wbf', bs)] + [('hm', kt, b) for b in range(NBLK)], w=[('ps', 4 + (k4 % 2))])
                P.add('act', lambda e, bk=bk, k4=k4: e.copy(out=utm[:, :, k4 * 4:(k4 + 1) * 4, :].rearrange("p g k i -> p k g i"), in_=bk[:, :].rearrange("p (k g i) -> p k g i", k=4, g=8)),
                      r=[('ps', 4 + (k4 % 2))], w=[('utm',)])
            ubf = Sprev[:].rearrange("p a b c d -> p (a b c d)").rearrange("p (g x) -> p g x", g=8)
            P.add('act', lambda e: e.copy(out=ubf, in_=utm[:].rearrange("p g k i -> p g (k i)")), r=[('utm',)], w=[('Sprev',)])
            for gl in range(8):
                for kh in range(2):
                    qd = (gl * 2 + kh) % 4
                    P.add('pe', lambda e, gl=gl, kh=kh, qd=qd: e.matmul(
                        banks[6][:, qd * 128:(qd + 1) * 128], ubf[:, gl, kh * 128:(kh + 1) * 128], identb[:, :],
                        start=True, stop=True), r=[('Sprev',), ('identb',)], w=[('ps', 6, qd)])
                    P.add('act', lambda e, gl=gl, kh=kh, qd=qd: e.copy(out=U[:, kh, gl, :], in_=banks[6][:, qd * 128:(qd + 1) * 128]),
                          r=[('ps', 6, qd)], w=[('U',)])
            P.add('pool', lambda e: e.memset(SprevZ[:], 0.0), r=[], w=[('Sprev',)])
            P.add('dve', lambda e: e.tensor_tensor(out=utm[:], in0=utm[:], in1=dskt.rearrange("p (g i) -> p g i", g=8).unsqueeze(2).broadcast_to([128, 8, 16, 16]), op=ALU.mult),
                  r=[('utm',), ('dskt',), ('U',)], w=[('utm',)])

            def plane(e, out, Ptab, sl0, vr, vi, sign_im, rows=slice(0, 128), neg=False, d=0, g2l=0, tmp=None, onpool=False):
                tmp = t1 if tmp is None else tmp
                Pr = Ptab[rows, 0, d, g2l, sl0:sl0 + 16].unsqueeze(2).broadcast_to([rows.stop - rows.start, 16, 16])
                Pi = Ptab[rows, 1, d, g2l, sl0:sl0 + 16].unsqueeze(2).broadcast_to([rows.stop - rows.start, 16, 16])
                n = rows.stop - rows.start
                vrb = vr.unsqueeze(1).broadcast_to([n, 16, 16])
                vib = vi.unsqueeze(1).broadcast_to([n, 16, 16])
                x0 = tmp[rows, 0].rearrange("p (a b) -> p a b", a=16)
                x1 = tmp[rows, 1].rearrange("p (a b) -> p a b", a=16)
                o_re = out[0].rearrange("p (a b) -> p a b", a=16)
                o_im = out[1].rearrange("p (a b) -> p a b", a=16)
                tt(e, x0, Pr, vrb, ALU.mult)
                tt(e, x1, Pi, vib, ALU.mult)
                tt(e, o_re, x0, x1, ALU.subtract)
                tt(e, x0, Pr, vib, ALU.mult)
                tt(e, x1, Pi, vrb, ALU.mult)
                if neg and onpool:
                    e.tensor_scalar(out=x0, in0=x0, scalar1=-1.0, scalar2=None, op0=ALU.mult)
                    return tt(e, o_im, x0, x1, ALU.subtract)
                if neg:
                    return e.scalar_tensor_tensor(out=o_im, in0=x0, scalar=-1.0, in1=x1, op0=ALU.mult, op1=ALU.subtract)
                return tt(e, o_im, x0, x1, ALU.add)

            def tabs(d):
                tabG, offG = (Pd, 1) if d == 0 else (Pa, 15)
                tabH, offH = (Pa, 0) if d == 0 else (Pd, 16)
                tabC, offC = (Pa, 16) if d == 0 else (Pd, 0)
                return tabG, offG, tabH, offH, tabC, offC

            for d in range(2):
                for g2l in range(4):
                    slot = bidx % 2
                    bidx += 1
                    tabG, offG, tabH, offH, tabC, offC = tabs(d)

                    def genA(e, d=d, g2l=g2l, slot=slot, tabG=tabG, offG=offG):
                        return plane(e, (Gp[:, slot, 0], Gp[:, slot, 1]), tabG, offG, bbp[:, 0, d, g2l, :], bbp[:, 1, d, g2l, :], 1, d=d, g2l=g2l)
                    P.seq('dve', genA, r=KE, w=[('Gp', slot), ('t1',)], nosync=True)
                    for gp in range(2):
                        gl = g2l * 2 + gp
                        rows = slice(gp * 64, gp * 64 + 64)
                        fb = 2 + gp
                        for kh in range(2):
                            for ri in range(2):
                                P.add('pe', lambda e, fb=fb, kh=kh, ri=ri, rows=rows, slot=slot: e.matmul(
                                    banks[fb][:, (kh * 2 + ri) * 64:(kh * 2 + ri + 1) * 64], Gp[rows, slot, ri, kh * 128:(kh + 1) * 128],
                                    identb[rows, rows], start=True, stop=True), r=[('Gp', slot), ('identb',)], w=[('ps', fb)])
                        fs = gp
                        P.add('act', lambda e, fb=fb, fs=fs: e.copy(out=Fm[:, fs].rearrange("p a b c -> p (a b c)"), in_=banks[fb][:, 0:256]),
                              r=[('ps', fb)], w=[('Fm', fs)])
                        for ri in range(2):
                            for kh in range(2):
                                P.add('pe', lambda e, ri=ri, kh=kh, rows=rows, fs=fs, gl=gl, d=d: e.matmul(
                                    banks[7][rows, (d * 2 + ri) * 128:(d * 2 + ri + 1) * 128], Fm[:, fs, kh, ri, :], U[:, kh, gl, :],
                                    start=(kh == 0), stop=(kh == 1)), r=[('Fm', fs), ('U',)], w=[('ps', 7, d)])
                    for ri in range(2):
                        P.add('act', lambda e, ri=ri, d=d, g2l=g2l: e.copy(
                            out=SL[:, ri, g2l, d * 128:(d + 1) * 128], in_=banks[7][:, (d * 2 + ri) * 128:(d * 2 + ri + 1) * 128]),
                            r=[('ps', 7, d)], w=[('SL',)])
            def cap(base, off, dims):
                if not hasattr(base, 'tensor'):
                    base = base[:]
                return bass.AP(tensor=base.tensor, offset=base.offset + off, ap=[list(base.ap[0])] + [list(x) for x in dims])

            def cmulblk(e, o_re, o_im, i_re, i_im, s_re, s_im, tmp):
                m1, m2 = tmp
                tt(e, m1, i_re, s_re, ALU.mult)
                tt(e, m2, i_im, s_im, ALU.mult)
                tt(e, o_re, m1, m2, ALU.subtract)
                tt(e, m1, i_re, s_im, ALU.mult)
                tt(e, m2, i_im, s_re, ALU.mult)
                tt(e, o_im, m1, m2, ALU.add)

            def wfn(e):
                for ri in range(2):
                    e.tensor_copy(out=Wt[:, ri, 0, :, 0], in_=Pa[:, ri, 0, :, 31])
                    e.tensor_copy(out=Wt[:, ri, 1, :, 15], in_=Pa[:, ri, 1, :, 31])
                for n in (1, 2, 4, 8):
                    tmpv = (wtmp[:, 0, :, 0:n], wtmp[:, 1, :, 0:n])
                    bc = lambda ap_: ap_.unsqueeze(2).broadcast_to([128, 4, n])
                    cmulblk(e, Wt[:, 0, 0, :, n:2 * n], Wt[:, 1, 0, :, n:2 * n], Wt[:, 0, 0, :, 0:n], Wt[:, 1, 0, :, 0:n],
                            bc(Wt[:, 0, 0, :, n - 1]), bc(Wt[:, 1, 0, :, n - 1]), tmpv)
                    cmulblk(e, Wt[:, 0, 1, :, 16 - 2 * n:16 - n], Wt[:, 1, 1, :, 16 - 2 * n:16 - n], Wt[:, 0, 1, :, 16 - n:16], Wt[:, 1, 1, :, 16 - n:16],
                            bc(Wt[:, 0, 1, :, 16 - n]), bc(Wt[:, 1, 1, :, 16 - n]), tmpv)
                for ri_t, dst16, dst256 in ((0, ArX, A2rX), (1, AiX, A2iX)):
                    for ri in range(2):
                        e.tensor_copy(out=dst16[:, ri * 4:(ri + 1) * 4, :], in_=Pa[:, ri_t, :, :, 31].rearrange("p d g -> p g d"))
                        e.tensor_copy(out=dst256[:, ri * 4:(ri + 1) * 4, 0], in_=Wt[:, ri_t, 0, :, 15])
                        e.tensor_copy(out=dst256[:, ri * 4:(ri + 1) * 4, 1], in_=Wt[:, ri_t, 1, :, 0])
                for ri in range(2):
                    e.tensor_copy(out=Bnd[:, 0, ri * 4:(ri + 1) * 4, :], in_=s5i[:, ri, g2s, :])
            P.seq('pool', wfn, r=KE + [('s5i',)], w=[('Wt',)])

            def scanfn(e):
                T1, T2, T3 = scn[:, 0], scn[:, 1], scn[:, 2]
                bcb = lambda t: cap(t, 0, [[2, 8], [1, 2], [0, 8]])
                for n in range(1, 16):
                    cur = cap(SL, n, [[256, 8], [143 - 2 * n, 2], [16, 8]])
                    prev = cap(SL, n - 1, [[256, 8], [145 - 2 * n, 2], [16, 8]])
                    tt(e, T1, prev, bcb(ArX), ALU.mult)
                    tt(e, T2, prev, bcb(AiX), ALU.mult)
                    tt(e, T3[:, 0:4], T1[:, 0:4], T2[:, 4:8], ALU.subtract)
                    tt(e, T3[:, 4:8], T1[:, 4:8], T2[:, 0:4], ALU.add)
                    tt(e, cur, cur, T3, ALU.add)
                C1, C2, C3 = scn2[:, 0], scn2[:, 1], scn2[:, 2]
                for n in range(8):
                    send = cap(SL, 16 * n + 15, [[256, 8], [225 - 32 * n, 2]])
                    mkv = cap(maskdc, 16 * n, [[0, 8], [255 - 32 * n, 2]])
                    cin = cap(CinB, n, [[16, 8], [15 - 2 * n, 2]])
                    tt(e, cin, Bnd[:, n], mkv, ALU.mult)
                    tt(e, C1, cin, A2rX[:], ALU.mult)
                    tt(e, C2, cin, A2iX[:], ALU.mult)
                    tt(e, C3[:, 0:4], C1[:, 0:4], C2[:, 4:8], ALU.subtract)
                    tt(e, C3[:, 4:8], C1[:, 4:8], C2[:, 0:4], ALU.add)
                    tt(e, Bnd[:, n + 1], send, C3, ALU.add)
                mt = mtb.rearrange("p (g b j) -> p g b j", g=4, b=8)
                for d in range(2):
                    Sv = [cap(SL, ri * 1024 + d * 128, [[256, 4], [16, 8], [1, 16]]) for ri in range(2)]
                    Wv = [cap(Wt, ri * 128 + d * 64, [[16, 4], [0, 8], [1, 16]]) for ri in range(2)]
                    Cv = [cap(CinB, ri * 64 + d * 8, [[16, 4], [1, 8], [0, 16]]) for ri in range(2)]
                    tt(e, mt, Wv[0], Cv[0], ALU.mult)
                    tt(e, Sv[0], Sv[0], mt, ALU.add)
                    tt(e, mt, Wv[1], Cv[1], ALU.mult)
                    tt(e, Sv[0], Sv[0], mt, ALU.subtract)
                    tt(e, mt, Wv[0], Cv[1], ALU.mult)
                    tt(e, Sv[1], Sv[1], mt, ALU.add)
                    tt(e, mt, Wv[1], Cv[0], ALU.mult)
                    tt(e, Sv[1], Sv[1], mt, ALU.add)
                for d in range(2):
                    for gp in range(2):
                        rows = slice(gp * 64, gp * 64 + 64)
                        Z = SprevZ[rows, :, :, d, gp, :].rearrange("p a b c -> p (a b) c").rearrange("p r (b j) -> p r b j", j=16)
                        Sd = SL[rows, :, :, d * 128:(d + 1) * 128].rearrange("p a b c -> p (a b) c").rearrange("p r (b j) -> p r b j", j=16)
                        if d == 0:
                            e.tensor_copy(out=Z[:, :, :, 1:16], in_=Sd[:, :, :, 0:15])
                            e.tensor_copy(out=Z[:, :, :, 0], in_=CinB[rows, :, 0, :])
                        else:
                            e.tensor_copy(out=Z[:, :, :, 0:15], in_=Sd[:, :, :, 1:16])
                            e.tensor_copy(out=Z[:, :, :, 15], in_=CinB[rows, :, 1, :])
            n_scan0 = len(P.ops)
            P.seq('dve', scanfn, r=[('s5i',), ('maskdc',), ('Wt',)] + KE, w=[('SL',), ('scn',), ('Sprev',), ('mtb',)])
            scan_ops = P.ops[n_scan0:]
            del P.ops[n_scan0:]
            n_b10 = len(P.ops)
            for g2l in range(4):
                yb = 2 + g2l
                for d in range(2):
                    mk = mkF if d == 0 else mkB
                    slot = bidx % 2
                    bidx += 1
                    tabG, offG, tabH, offH, tabC, offC = tabs(d)

                    def genB(e, d=d, g2l=g2l, slot=slot, tabG=tabG, offG=offG, tabH=tabH, offH=offH, tabC=tabC, offC=offC):
                        plane(e, (Gp[:, slot, 0], Gp[:, slot, 1]), tabG, offG, bbp[:, 0, d, g2l, :], bbp[:, 1, d, g2l, :], 1, d=d, g2l=g2l)
                        plane(e, (Cma[:, d, g2l, 0], Cma[:, d, g2l, 1]), tabC, offC, bcp[:, 2, g2l, :], bcp[:, 3, g2l, :], -1, neg=True, d=d, g2l=g2l)

                    def genH(e, d=d, g2l=g2l, slot=slot, tabH=tabH, offH=offH):
                        plane(e, (Hp[:, slot, 0], Hp[:, slot, 1]), tabH, offH, bcp[:, 2, g2l, :], bcp[:, 3, g2l, :], -1, neg=True, d=d, g2l=g2l,
                              tmp=t1p, onpool=True)
                    P.seq('pool', genH, r=KE + [('s5bc', par_)], w=[('Hp', slot), ('t1p',)], nosync=True)
                    P.seq('dve', genB, r=KE + [('s5bc', par_)], w=[('Gp', slot), ('Cma', d, g2l), ('t1',)], nosync=True)
                    for gp in range(2):
                        gl = g2l * 2 + gp
                        rows = slice(gp * 64, gp * 64 + 64)
                        kb = 0 + gp
                        for kh in range(2):
                            for ri in range(2):
                                P.add('pe', lambda e, kb=kb, kh=kh, ri=ri, rows=rows, slot=slot: e.matmul(
                                    banks[kb][:, kh * 256:(kh + 1) * 256], Gp[rows, slot, ri, kh * 128:(kh + 1) * 128], Hp[rows, slot, ri, :],
                                    start=(ri == 0), stop=(ri == 1)), r=[('Gp', slot), ('Hp', slot)], w=[('ps', kb)])
                        P.add('dve', lambda e, kb=kb, d=d, gp=gp, mk=mk: e.tensor_tensor(
                            out=Kall[:, d, gp], in0=banks[kb][:, :].rearrange("p (a b) -> p a b", a=2), in1=mk[:], op=ALU.mult),
                            r=[('ps', kb), ('mkF',), ('mkB',)], w=[('Kall', d, gp)])
                        for kh in range(2):
                            first = (d == 0 and gp == 0 and kh == 0)
                            P.add('pe', lambda e, yb=yb, d=d, kh=kh, gl=gl, gp=gp, first=first: e.matmul(
                                banks[yb][:, gp * 256:(gp + 1) * 256], U[:, kh, gl, :], Kall[:, d, gp, kh, :], start=first, stop=False,
                                skip_group_check=True), r=[('U',), ('Kall', d, gp)], w=[('ps', yb)])
            b1_ops = P.ops[n_b10:]
            del P.ops[n_b10:]
            merged = []
            si_ = 0
            for op_ in b1_ops:
                merged.append(op_)
                if op_['eng'] == 'dve' and si_ < len(scan_ops):
                    merged.append(scan_ops[si_])
                    si_ += 1
            merged.extend(scan_ops[si_:])
            P.ops.extend(merged)
            stg = scn[:, 0]

            def stgfn(e):
                e.copy(out=stg[:, :, 0, :], in_=cap(SL, 15, [[256, 8], [16, 8]]))
                e.copy(out=stg[:, :, 1, :], in_=cap(SL, 128, [[256, 8], [16, 8]]))
            P.seq('act', stgfn, r=[('SL',)], w=[('scn',)])
            dma(st5_out[:, eg], stg, [('st5', eg)], r=[('scn',)], stream='st5')
            for g2l in range(4):
                yb = 2 + g2l
                for gp in range(2):
                    gl = g2l * 2 + gp
                    cnt = 0
                    for d in range(2):
                        for ri in range(2):
                            P.add('pe', lambda e, yb=yb, d=d, ri=ri, gp=gp, g2l=g2l, cnt=cnt: e.matmul(
                                banks[yb][:, gp * 256:(gp + 1) * 256], SprevZ[:, ri, g2l, d, gp, :], Cma[:, d, g2l, ri, :], start=False, stop=(cnt == 3),
                                skip_group_check=True), r=[('Sprev',), ('Cma', d, g2l)], w=[('ps', yb)])
                            cnt += 1
                for gp in range(2):
                    gl = g2l * 2 + gp
                    P.add('dve', lambda e, yb=yb, gl=gl, gp=gp: e.tensor_tensor(
                        out=utm[:, gl], in0=banks[yb][:, gp * 256:(gp + 1) * 256].rearrange("p (a b) -> p a b", a=16),
                        in1=utm[:, gl], op=ALU.add), r=[('ps', yb), ('utm',)], w=[('utm',)])
            gtmp = SL[:].rearrange("p a b c -> p (a b c)")
            utf = utm[:].rearrange("p g k i -> p (g k i)")
            gtmp4 = gtmp.rearrange("p (g t j) -> p g t j", g=8, t=16)
            P.add('act', lambda e: e.activation(out=gtmp, in_=utf, func=AF.Square), r=[('utm',), ('SL',), ('Sprev',)], w=[('SL',)])
            P.add('dve', lambda e: e.tensor_scalar(out=gtmp, in0=gtmp, scalar1=0.044715, scalar2=1.0, op0=ALU.mult, op1=ALU.add), r=[('SL',)], w=[('SL',)])
            P.add('dve', lambda e: e.tensor_tensor(out=gtmp, in0=gtmp, in1=utf, op=ALU.mult), r=[('SL',), ('utm',)], w=[('SL',)])
            P.add('act', lambda e: e.activation(out=gtmp, in_=gtmp, func=AF.Sigmoid, scale=1.5957691216), r=[('SL',)], w=[('SL',)])
            P.add('dve', lambda e: e.tensor_tensor(out=ytm[:].rearrange("p t (g j) -> p g t j", g=8), in0=gtmp4, in1=utm[:], op=ALU.mult), r=[('SL',), ('utm',)], w=[('Sprev',)])
            for t4 in range(4):
                for tq in range(4):
                    tau = t4 * 4 + tq
                    P.add('pe', lambda e, tq=tq, tau=tau: e.matmul(banks[6][:, tq * 128:(tq + 1) * 128], ytm[:, tau, :], identb[:, :],
                                                                    start=True, stop=True), r=[('Sprev',), ('identb',)], w=[('ps', 6, tq)])
                P.add('act', lambda e, t4=t4, eg=eg: e.copy(
                    out=yfm[:, eg, :].rearrange("p (c t) -> p t c", t=16)[:, t4 * 4:(t4 + 1) * 4, :],
                    in_=banks[6][:, :].rearrange("p (a b) -> p a b", a=4)),
                    r=[('ps', 6, q) for q in range(4)], w=[('yfm', eg)])
        gt = gtv[(layer, sub)]
        sgt, o2 = carve(ld.end, [2, BLK])
        mvt, o2 = carve(o2, [2, BLK])
        pc = 0
        for m in range(KT):
            ba_ = ld.load(wsrc(s5_w_glu, m * 128))
            bg_ = ld.load(wsrc(s5_w_glu, D + m * 128))
            for b in range(NBLK):
                pp = pc % 2
                pc += 1
                for (bs, bank) in ((ba_, pp * 2), (bg_, pp * 2 + 1)):
                    for kt in range(KT):
                        P.add('pe', lambda e, bs=bs, bank=bank, kt=kt, b=b: e.matmul(
                            banks[bank][:, :], ld.bf[:, bs, kt, :], yfm[:, kt, b * BLK:(b + 1) * BLK], start=(kt == 0), stop=(kt == KT - 1)),
                            r=[('s5wbf', bs), ('yfm', kt)], w=[('ps', bank)])
                P.add('act', lambda e, pp=pp: e.activation(out=sgt[:, pp], in_=banks[pp * 2 + 1][:, :], func=AF.Sigmoid),
                      r=[('ps', pp * 2 + 1)], w=[('sgt', pp)])
                P.add('dve', lambda e, pp=pp: e.tensor_tensor(out=mvt[:, pp], in0=banks[pp * 2][:, :], in1=sgt[:, pp], op=ALU.mult),
                      r=[('ps', pp * 2), ('sgt', pp)], w=[('mvt', pp)])
                P.add('dve', lambda e, pp=pp, m=m, b=b: e.scalar_tensor_tensor(
                    out=xs[:, m, b * BLK:(b + 1) * BLK], in0=mvt[:, pp], scalar=gt[:, m:m + 1], in1=xs[:, m, b * BLK:(b + 1) * BLK],
                    op0=ALU.mult, op1=ALU.add), r=[('mvt', pp), ('xs', m, b), ('sm', 'gt', layer, sub)], w=[('xs', m, b)])
        for b in range(NBLK):
            layernorm(b, ln_idx, o2)


    def lru_mixer():
        layer, sub, ln_idx = 1, 1, 4
        o = LNB
        yfm, o = carve(o, [KT, T], BF16)
        ld = Loader('lruw', o, 1, 2)
        o = ld.end
        gwb, o = carve(o, [4, KT, 128], BF16)
        cv, o = carve(o, [5, KT])
        cvb, o = carve(o, [4, KT])
        gpp, o = carve(o, [2, 3, KT])
        li, o = carve(o, [2, KT])
        sc8, o = carve(o, [2, 2, KT])
        Bf = []
        for i in range(5):
            bfi, o = carve(o, [T])
            Bf.append(bfi)
        xcb, o = carve(o, [T], BF16)
        Tt, o = carve(o, [2, 4, 128], BF16)
        lob, o = carve(o, [512], BF16)
        gg, o = carve(o, [2, BLK])
        assert o <= SCRN, o
        dma(cv[:], lru_conv[:, :, :], [('cv',)], stream='cv')
        dma(gpp[:], lru_gp[:, :, :, :], [('gpp',)], stream='gpp')
        dma(li[:], lruinit_in[:, :, :], [('li',)], stream='li')
        for i in range(4):
            dma(ld.st[:, 0], lru_gw[:, i, :, :], [('lruwst', 0)], stream='lruwst0')
            P.add('pool', lambda e, i=i: e.tensor_copy(out=gwb[:, i], in_=ld.st[:, 0]), r=[('lruwst', 0)], w=[('gwb',)])
        P.add('dve', lambda e: e.tensor_scalar(out=cvb[:], in0=cv[:, 0:4, :], scalar1=flags[:, 1:2], scalar2=None, op0=ALU.mult),
              r=[('cv',), ('flags',)], w=[('cvb',)])
        lamv = gpp[:, :, 2, :]
        P.add('act', lambda e: e.activation(out=sc8[:, :, 0, :], in_=lamv, func=AF.Exp, scale=-1.0), r=[('gpp',)], w=[('sc8',)])
        P.add('act', lambda e: e.activation(out=sc8[:, :, 0, :], in_=sc8[:, :, 0, :], func=AF.Ln, bias=1.0), r=[('sc8',)], w=[('sc8',)])
        P.add('dve', lambda e: e.tensor_scalar(out=sc8[:, :, 1, :], in0=sc8[:, :, 0, :], scalar1=-16.0, scalar2=None, op0=ALU.mult), r=[('sc8',)], w=[('sc8',)])
        P.add('dve', lambda e: e.tensor_scalar(out=sc8[:, :, 0, :], in0=sc8[:, :, 0, :], scalar1=-8.0, scalar2=None, op0=ALU.mult), r=[('sc8',)], w=[('sc8',)])

        def sv(buf):
            return buf.rearrange("p (s t) -> p s t", t=256)

        def reverse(src, dst, skey, dkey):
            P.add('dve', lambda e: e.tensor_copy(out=xcb, in_=src), r=[skey], w=[('xcb',)])
            for q in range(4):
                qs = slice(q * 512, (q + 1) * 512)
                P.add('dve', lambda e, qs=qs: e.tensor_tensor(out=lob, in0=src[:, qs], in1=xcb[:, qs], op=ALU.subtract),
                      r=[skey, ('xcb',)], w=[('lob',)])
                for hl, (sbuf_, bank) in enumerate(((xcb, 6), (lob, 7))):
                    for i in range(4):
                        col = (q * 512 + i * 128) if hl == 0 else i * 128
                        P.add('pe', lambda e, sbuf_=sbuf_, bank=bank, i=i, col=col: e.matmul(
                            banks[bank][:, i * 128:(i + 1) * 128], sbuf_[:, col:col + 128], identb[:, :], start=True, stop=True),
                            r=[('xcb',), ('lob',), ('identb',)], w=[('ps', bank)])
                    P.add('act', lambda e, hl=hl, bank=bank: e.copy(out=Tt[:, hl], in_=banks[bank][:, :].rearrange("p (a b) -> p a b", a=4)),
                          r=[('ps', bank)], w=[('Tt', hl)])
                for i in range(4):
                    for hl in range(2):
                        P.add('pe', lambda e, i=i, hl=hl: e.matmul(banks[5][:, (3 - i) * 128:(4 - i) * 128], Tt[:, hl, i, :], Jb[:, :],
                                                                  start=(hl == 0), stop=(hl == 1)),
                              r=[('Tt', hl), ('Jb',)], w=[('ps', 5)])
                P.add('act', lambda e, q=q: e.copy(out=dst[:, (12 - 4 * q) * 128:(16 - 4 * q) * 128], in_=banks[5][:, :]),
                      r=[('ps', 5)], w=[dkey])

        def lru_kt(kt):
            B0, B1, B2, B3, B4 = Bf
            kB = [('B', i) for i in range(5)]
            bx_ = ld.load(wsrc(lru_w_in, kt * 128))
            w = lambda k: cv[:, k, kt:kt + 1]
            wb = lambda k: cvb[:, k, kt:kt + 1]
            for b in range(NBLK):
                bank = b % 2
                cs = slice(b * BLK, (b + 1) * BLK)
                for k in range(KT):
                    P.add('pe', lambda e, bank=bank, k=k, cs=cs, bx_=bx_: e.matmul(banks[bank][:, :], ld.bf[:, bx_, k, :], hm[:, k, cs],
                                                                                 start=(k == 0), stop=(k == KT - 1)),
                          r=[('lruwbf', bx_), ('hm', k, b)], w=[('ps', bank)])
                P.add('act', lambda e, bank=bank, cs=cs: e.activation(out=B1[:, cs], in_=banks[bank][:, :], func=AF.Identity, scale=w(2), bias=w(4)),
                      r=[('ps', bank), ('cv',)], w=[kB[1]])
                P.add('dve', lambda e, bank=bank, cs=cs: e.tensor_copy(out=B0[:, cs], in_=banks[bank][:, :]), r=[('ps', bank)], w=[kB[0]])
            bg_ = ld.load(wsrc(lru_w_in, D + kt * 128))

            def convfn(e):
                xr, xc = sv(B0), sv(B1)
                stt = lambda out, in0, sc: e.scalar_tensor_tensor(out=out, in0=in0, scalar=sc, in1=out, op0=ALU.mult, op1=ALU.add)
                stt(xc[:, :, 2:256], xr[:, :, 0:254], w(0))
                stt(xc[:, :, 1:256], xr[:, :, 0:255], w(1))
                stt(xc[:, :, 0:255], xr[:, :, 1:256], w(3))
                stt(xc[:, 1:8, 0:1], xr[:, 0:7, 254:255], wb(0))
                stt(xc[:, 1:8, 0:1], xr[:, 0:7, 255:256], wb(1))
                stt(xc[:, 1:8, 1:2], xr[:, 0:7, 255:256], wb(0))
                return stt(xc[:, 0:7, 255:256], xr[:, 1:8, 0:1], wb(3))
            P.seq('dve', convfn, r=[kB[0], ('cv',), ('cvb',)], w=[kB[1]])
            reverse(B1, B2, kB[1], kB[2])
            for d in range(2):
                src, skey = (B1, kB[1]) if d == 0 else (B2, kB[2])
                R, rkey = (B0, kB[0]) if d == 0 else (B1, kB[1])
                GI, A = B3, B4
                if d == 1:
                    P.add('dve', lambda e, src=src: e.tensor_copy(out=xcb, in_=src), r=[skey], w=[('xcb',)])
                for b in range(NBLK):
                    cs = slice(b * BLK, (b + 1) * BLK)
                    for gi_, (dstb, dkey) in enumerate(((R, rkey), (GI, kB[3]))):
                        bank = 2 + gi_
                        P.add('pe', lambda e, bank=bank, d=d, gi_=gi_, cs=cs: e.matmul(banks[bank][:, :], gwb[:, d * 2 + gi_, kt, :], xcb[:, cs],
                                                                                       start=True, stop=True),
                              r=[('gwb',), ('xcb',)], w=[('ps', bank)])
                        P.add('act', lambda e, bank=bank, d=d, gi_=gi_, cs=cs, dstb=dstb: e.activation(
                            out=dstb[:, cs], in_=banks[bank][:, :], func=AF.Sigmoid, bias=gpp[:, d, gi_, kt:kt + 1]),
                            r=[('ps', bank), ('gpp',)], w=[dkey])
                P.add('act', lambda e, d=d, R=R: e.activation(out=A, in_=R, func=AF.Exp, scale=sc8[:, d, 0, kt:kt + 1]), r=[rkey, ('sc8',)], w=[kB[4]])
                P.add('act', lambda e, d=d, R=R: e.activation(out=R, in_=R, func=AF.Exp, scale=sc8[:, d, 1, kt:kt + 1]), r=[rkey, ('sc8',)], w=[rkey])
                P.add('dve', lambda e, R=R: e.tensor_scalar(out=R, in0=R, scalar1=-1.0, scalar2=1.0, op0=ALU.mult, op1=ALU.add), r=[rkey], w=[rkey])
                P.add('act', lambda e, R=R: e.activation(out=R, in_=R, func=AF.Sqrt), r=[rkey], w=[rkey])
                P.add('dve', lambda e, R=R: e.tensor_tensor(out=GI, in0=GI, in1=R, op=ALU.mult), r=[rkey, kB[3]], w=[kB[3]])
                P.add('dve', lambda e, src=src: e.tensor_tensor(out=GI, in0=GI, in1=src, op=ALU.mult), r=[skey, kB[3]], w=[kB[3]])
                P.add('dve', lambda e: e.tensor_scalar(out=sv(A)[:, 1:8, 0:1], in0=sv(A)[:, 1:8, 0:1], scalar1=flags[:, 1:2], scalar2=None, op0=ALU.mult),
                      r=[kB[4], ('flags',)], w=[kB[4]])
                P.add('dve', lambda e, d=d, R=R: e.tensor_tensor_scan(out=R, data0=A, data1=GI, initial=li[:, d, kt:kt + 1], op0=ALU.mult, op1=ALU.add),
                      r=[kB[4], kB[3], ('li',)], w=[rkey])
            reverse(B1, B2, kB[1], kB[2])
            dma(stl_out[:, 0, kt, :], sv(B0)[:, :, 255], [('stl', kt, 0)], r=[kB[0]], stream='stl', slow=True)
            dma(stl_out[:, 1, kt, :], sv(B2)[:, :, 0], [('stl', kt, 1)], r=[kB[2]], stream='stl', slow=True)
            P.add('dve', lambda e: e.tensor_tensor(out=B3, in0=B0, in1=B2, op=ALU.add), r=[kB[0], kB[2]], w=[kB[3]])
            for b in range(NBLK):
                bank = b % 2
                cs = slice(b * BLK, (b + 1) * BLK)
                sl = b % 2
                for k in range(KT):
                    P.add('pe', lambda e, bank=bank, k=k, cs=cs, bg_=bg_: e.matmul(banks[bank][:, :], ld.bf[:, bg_, k, :], hm[:, k, cs],
                                                                                 start=(k == 0), stop=(k == KT - 1)),
                          r=[('lruwbf', bg_), ('hm', k, b)], w=[('ps', bank)])
                P.add('act', lambda e, bank=bank, sl=sl: e.activation(out=gg[:, sl], in_=banks[bank][:, :], func=AF.Square), r=[('ps', bank)], w=[('gg', sl)])
                P.add('dve', lambda e, sl=sl: e.tensor_scalar(out=gg[:, sl], in0=gg[:, sl], scalar1=0.044715, scalar2=1.0, op0=ALU.mult, op1=ALU.add),
                      r=[('gg', sl)], w=[('gg', sl)])
                P.add('dve', lambda e, bank=bank, sl=sl: e.tensor_tensor(out=gg[:, sl], in0=gg[:, sl], in1=banks[bank][:, :], op=ALU.mult),
                      r=[('gg', sl), ('ps', bank)], w=[('gg', sl)])
                P.add('act', lambda e, sl=sl: e.activation(out=gg[:, sl], in_=gg[:, sl], func=AF.Sigmoid, scale=1.5957691216), r=[('gg', sl)], w=[('gg', sl)])
                P.add('dve', lambda e, bank=bank, sl=sl: e.tensor_tensor(out=gg[:, sl], in0=gg[:, sl], in1=banks[bank][:, :], op=ALU.mult),
                      r=[('gg', sl), ('ps', bank)], w=[('gg', sl)])
                P.add('dve', lambda e, sl=sl, cs=cs, kt=kt: e.tensor_tensor(out=yfm[:, kt, cs], in0=gg[:, sl], in1=B3[:, cs], op=ALU.mult),
                      r=[('gg', sl), kB[3]], w=[('yfm', kt)])
        for kt_ in range(KT):
            lru_kt(kt_)
        gt = gtv[(layer, sub)]
        pc = 0
        for m in range(KT):
            bw_ = ld.load(wsrc(lru_w_out, m * 128))
            for b in range(NBLK):
                bank = 4 + (pc % 2)
                pc += 1
                cs = slice(b * BLK, (b + 1) * BLK)
                for k in range(KT):
                    P.add('pe', lambda e, bank=bank, k=k, cs=cs, bw_=bw_: e.matmul(banks[bank][:, :], ld.bf[:, bw_, k, :], yfm[:, k, cs],
                                                                                 start=(k == 0), stop=(k == KT - 1)),
                          r=[('lruwbf', bw_), ('yfm', k)], w=[('ps', bank)])
                P.add('dve', lambda e, bank=bank, m=m, cs=cs, b=b: e.scalar_tensor_tensor(
                    out=xs[:, m, cs], in0=banks[bank][:, :], scalar=gt[:, m:m + 1], in1=xs[:, m, cs], op0=ALU.mult, op1=ALU.add),
                    r=[('ps', bank), ('xs', m, b), ('sm', 'gt', layer, sub)], w=[('xs', m, b)])
        P.barrier(dummy[:])
        for b in range(NBLK):
            layernorm(b, ln_idx, 0)


    env = dict(locals())
    P.barrier(dummy[:])
    def ada1_hook(f):
        for half in range(2):
            ch = f * 2 + half
            if ch < 36:
                ada_compute(1, ch, 256, 23200, 'adaB')
            if ch + 2 < 36:
                ada_load(1, ch + 2, 256, 23200, 'adaB')
    if STAGE >= 1:
        ada_load(1, 0, 256, 23200, 'adaB')
        ada_load(1, 1, 256, 23200, 'adaB')
        ffn(0, 0, 0, hook=ada1_hook)
        derive(1)
        P.barrier(dummy[:])
    if STAGE >= 2:
        s5_mixer()
        P.barrier(dummy[:])
    if STAGE >= 3:
        ffn(0, 2, 2)
        ffn(1, 0, 3, soft=True)
    if STAGE >= 4:
        lru_mixer()
    if STAGE >= 5:
        ffn(1, 2, 5)
    if STAGE < 99:
        for kt in range(KT):
            dma(y_out[kt * 128:(kt + 1) * 128, :], xs[:, kt, :], [('yout', kt)], r=[('xs', kt, b) for b in range(NBLK)], stream='yo%d' % (kt % 4))
    return nc, P, ctx


def finalize(nc, P, ctx):
    P.analyze()
    names = set(s for s in P.final.keys())
    sems = {}
    for s in sorted(names):
        c = nc.semaphore(s)
        ctx.append(c)
        sems[s] = c.__enter__()
    out_streams = [s for s in P.streams if s.startswith('dma_yo') or s.startswith('dma_st')]
    with nc.Block() as block:
        @block.tensor
        def _(e):
            P.emit('pe', e, sems)

        @block.scalar
        def _(e):
            P.emit('act', e, sems)

        @block.vector
        def _(e):
            P.emit('dve', e, sems)

        @block.gpsimd
        def _(e):
            P.emit('pool', e, sems)

        @block.sync
        def _(e):
            P.emit('sp', e, sems, out_streams)
    for c in reversed(ctx):
        c.__exit__(None, None, None)
    return nc


def host_inputs(inp):
    f = lambda a: np.ascontiguousarray(np.asarray(a, dtype=np.float32))
    xp = f(inp["x_prompt"]); xsmp = f(inp["x_sample"])
    c = f(inp["c"]); c_ctx = f(inp["c_ctx"])
    consts = np.zeros((128, 128 * 3 + 1024), np.float32)
    consts[:, 0:128] = np.eye(128)
    consts[:, 128:256] = 1.0
    consts[:, 256:384] = np.eye(128)[::-1]
    k = np.arange(128) // 16
    for kh in range(2):
        kk = kh * 8 + k
        tau = np.arange(256) // 16
        consts[:, 384 + kh * 256:384 + (kh + 1) * 256] = (tau[None, :] >= kk[:, None])
        consts[:, 896 + kh * 256:896 + (kh + 1) * 256] = (kk[:, None] >= tau[None, :])
    shared = {}
    shared["consts_in"] = consts
    shared["ada_w"] = f(inp["ada_w"])
    shared["ada_b"] = f(f(inp["ada_b"]).reshape(2, 72, 128).transpose(2, 0, 1))
    shared["ln_g"] = f(f(inp["ln_g"]).reshape(2, 3, 8, 128).transpose(3, 0, 1, 2))
    shared["ln_b"] = f(f(inp["ln_b"]).reshape(2, 3, 8, 128).transpose(3, 0, 1, 2))
    shared["w_up"] = f(inp["ffn_w_up"])
    shared["w_down"] = f(inp["ffn_w_down"])
    shared["s5_w_in"] = f(inp["s5_w_in"][0])
    shared["s5_w_glu"] = f(inp["s5_w_glu"][0])
    are = f(inp["s5_a_re"])[0]; aim = f(inp["s5_a_im"])[0]; ldt = f(inp["s5_log_dt"])[0]
    par = np.zeros((2, 64, 64, 3), np.float32)
    par[..., 0] = are; par[..., 1] = aim; par[..., 2] = ldt[:, :, None]
    par = par.reshape(2, 32, 2, 64, 3).transpose(2, 3, 0, 1, 4).reshape(128, 2, 32, 3)
    shared["s5_par"] = f(par)
    bre = f(inp["s5_b_re"])[0]; bim = f(inp["s5_b_im"])[0]
    cre = f(inp["s5_c_re"])[0].transpose(0, 2, 1); cim = f(inp["s5_c_im"])[0].transpose(0, 2, 1)
    bc = np.stack([bre, bim, cre, cim], 0)
    bc = bc.reshape(4, 32, 2, 64, 16).transpose(2, 3, 0, 1, 4).reshape(128, 4, 32, 16)
    shared["s5_bc"] = f(bc)
    shared["s5_dsk"] = f(np.broadcast_to(f(inp["s5_d"])[0][None, :], (128, D)))
    shared["lru_w_in"] = f(inp["lru_w_in"][0])
    shared["lru_w_out"] = f(inp["lru_w_out"][0])
    cw = f(inp["lru_conv_w"])[0]; cb = f(inp["lru_conv_b"])[0]
    conv = np.concatenate([cw, cb[None]], 0).reshape(5, 8, 128).transpose(2, 0, 1)
    shared["lru_conv"] = f(conv)
    wa = f(inp["lru_w_a"])[0]; wx = f(inp["lru_w_x"])[0]
    gw = np.zeros((128, 4, 8, 128), np.float32)
    for d in range(2):
        for gi_, wsrc in enumerate((wa, wx)):
            for h in range(16):
                kt, hp = h // 2, h % 2
                gw[hp * 64:(hp + 1) * 64, d * 2 + gi_, kt, hp * 64:(hp + 1) * 64] = wsrc[d, h]
    shared["lru_gw"] = gw
    ba = f(inp["lru_b_a"])[0]; bx = f(inp["lru_b_x"])[0]; lam = f(inp["lru_lambda"])[0]
    gp_ = np.stack([ba, bx, lam], 1).reshape(2, 3, 8, 128).transpose(3, 0, 1, 2)
    shared["lru_gp"] = f(gp_)
    st5 = f(inp["state_s5"]); stl = f(inp["state_lru"])
    maps = []
    for core in range(8):
        m = dict(shared)
        if core < 4:
            xc = xp[core * 8:(core + 1) * 8].reshape(T, D)
            cond = c_ctx
            fl = [0.0, 0.0, 0.0, 0.0]
            mk = np.ones(256, np.float32)
            mk[[16, 32, 48, 64, 80, 96, 112]] = 0.0
            mk[[128 + 15, 128 + 31, 128 + 47, 128 + 63, 128 + 79, 128 + 95, 128 + 111]] = 0.0
            s5i = np.zeros((128, 2, 32, 2), np.float32)
            lri = np.zeros((128, 2, 8), np.float32)
        else:
            bi = core - 4
            xc = xsmp[bi]
            cond = c[bi]
            fl = [1.0, 1.0, 0.0, 0.0]
            mk = np.ones(256, np.float32)
            s = st5[bi, 0]
            s5i = s.reshape(2, 2, 32, 2, 64).transpose(3, 4, 1, 2, 0).reshape(128, 2, 32, 2)
            lri = stl[bi, 0].reshape(2, 8, 128).transpose(2, 0, 1)
        m["x_in"] = f(xc.T)
        m["cond_in"] = f(cond.reshape(8, 128).T)
        m["flags_in"] = f(np.broadcast_to(np.array(fl, np.float32)[None], (128, 4)))
        m["maskdc_in"] = f(np.broadcast_to(mk[None], (128, 256)))
        m["s5init_in"] = f(s5i)
        m["lruinit_in"] = f(lri)
        maps.append(m)
    return maps


_CACHE = {}


def kernel(**inputs):
    maps = host_inputs(inputs)
    if "nc" not in _CACHE:
        nc, P, ctx = build()
        _CACHE["nc"] = finalize(nc, P, ctx)
    nc = _CACHE["nc"]
    used = set(t for t in maps[0].keys())
    res = run_bass_kernel_spmd(nc, maps, core_ids=list(range(8)))
    R = res.results
    y_p = np.stack([R[cidx]["y_out"].T.reshape(8, 256, D) for cidx in range(4)], 0).reshape(32, 256, D)
    y_s = np.stack([R[cidx]["y_out"].T for cidx in range(4, 8)], 0)
    s5 = np.zeros((32, 1, 2, 2, 64, 64), np.float32)
    sl = np.zeros((32, 1, 2, 1024), np.float32)
    for cidx in range(4):
        a = R[cidx]["st5_out"]
        a = a.reshape(2, 64, 8, 2, 4, 2, 8)
        a = a.transpose(6, 5, 3, 2, 4, 0, 1)
        s5[cidx * 8:(cidx + 1) * 8, 0] = a.reshape(8, 2, 2, 64, 64)
        b = R[cidx]["stl_out"]
        sl[cidx * 8:(cidx + 1) * 8, 0] = b.transpose(3, 1, 2, 0).reshape(8, 2, 1024)
    return (y_p.astype(np.float32), y_s.astype(np.float32), s5, sl)
```

```python
import math
import numpy as np
import concourse.bass as bass
import concourse.mybir as mybir
from concourse.bass_utils import run_bass_kernel_spmd

F32 = mybir.dt.float32
BF16 = mybir.dt.bfloat16
AF = mybir.ActivationFunctionType
ALU = mybir.AluOpType

D = 1024
T = 2048
KT = 8
NBLK = 4
BLK = 512
DFF = 2816
NF = 22
ALPHA = 4.0 ** 0.25
LN_EPS = 1e-5
SECTIONS = [(0, 6), (6, 12), (12, 17), (17, 22)]
STAGE = 99


class Prog:
    def __init__(self):
        self.ops = []

    def add(self, eng, fn, r=(), w=(), stream=None):
        r2, w2 = [], []
        for k in r:
            if k[0] == 'ps':
                w2.append(('ps', k[1]))
            else:
                r2.append(k)
        for k in w:
            w2.append(('ps', k[1]) if k[0] == 'ps' else k)
        self.ops.append(dict(eng=eng, fn=fn, r=r2, w=w2, stream=stream))

    def seq(self, eng, fn, r=(), w=(), nosync=False):
        prog = self
        r = list(r)
        w = list(w)

        prog._seqid = getattr(prog, '_seqid', 0) + 1
        sid = prog._seqid if nosync else None

        class Rec:
            def __getattr__(self, name):
                def call(*a, **k):
                    prog.add(eng, lambda e: getattr(e, name)(*a, **k), r=r, w=w)
                    prog.ops[-1]['sid'] = sid
                    return None
                return call
        fn(Rec())

    def barrier(self, tile):
        self.ops.append(dict(eng='dve', fn=(lambda e: e.memset(tile, 0.0)), r=[], w=[], stream=None, barrier=True))

    def analyze(self):
        ops = self.ops
        lastw = {}
        readers = {}
        deps = [set() for _ in ops]
        last_eng = {}
        last_bar = None
        for i, op in enumerate(ops):
            if op.get('barrier'):
                for m in last_eng.values():
                    deps[i].add(m)
                last_bar = i
                last_eng = {}
            else:
                if last_bar is not None:
                    deps[i].add(last_bar)
                last_eng[(op['eng'], op['stream'])] = i
            for k in op['r']:
                if k in lastw:
                    deps[i].add(lastw[k])
            for k in op['w']:
                if k in lastw:
                    deps[i].add(lastw[k])
                for rr in readers.get(k, ()):
                    deps[i].add(rr)
            for k in op['r']:
                readers.setdefault(k, []).append(i)
            for k in op['w']:
                lastw[k] = i
                readers[k] = []
            deps[i].discard(i)
        for i, op in enumerate(ops):
            keep = set()
            for m in deps[i]:
                om = ops[m]
                if om['stream'] is None and op['stream'] is None and om['eng'] == op['eng'] == 'pe':
                    continue
                if op.get('sid') is not None and om.get('sid') == op.get('sid'):
                    continue
                keep.add(m)
            deps[i] = keep
        signaling = [False] * len(ops)
        for i in range(len(ops)):
            for m in deps[i]:
                signaling[m] = True
        cnt = {}
        sig = [None] * len(ops)
        streams = []
        for i, op in enumerate(ops):
            if op['stream'] is not None:
                s = 'dma_' + op['stream']
                if s not in cnt:
                    streams.append(s)
                cnt[s] = cnt.get(s, 0) + 16
                sig[i] = (s, cnt[s])
            elif signaling[i]:
                s = 'eng_' + op['eng']
                cnt[s] = cnt.get(s, 0) + 1
                sig[i] = (s, cnt[s])
        self.deps = deps
        self.sig = sig
        self.final = cnt
        self.streams = streams

    def emit(self, engname, e, sems, out_streams=()):
        waited = {}
        for i, op in enumerate(self.ops):
            if op['eng'] != engname:
                continue
            need = {}
            for m in self.deps[i]:
                s, v = self.sig[m]
                if need.get(s, 0) < v:
                    need[s] = v
            for s, v in need.items():
                if waited.get(s, 0) < v:
                    e.wait_ge(sems[s], v)
                    waited[s] = v
            ins = op['fn'](e)
            if self.sig[i] is not None:
                s, v = self.sig[i]
                ins.then_inc(sems[s], 16 if op['stream'] is not None else 1)
        for s in out_streams:
            e.wait_ge(sems[s], self.final[s])


def build():
    nc = bass.Bass("TRN2", target_bir_lowering=False, dynamic_dma_scratch_size=512)
    P = Prog()

    def din(name, shape, dt=F32):
        return nc.dram_tensor(name, list(shape), dt, kind="ExternalInput").ap()

    def dout(name, shape, dt=F32):
        return nc.dram_tensor(name, list(shape), dt, kind="ExternalOutput").ap()

    x_in = din("x_in", [D, T])
    cond_in = din("cond_in", [128, KT])
    flags_in = din("flags_in", [128, 4])
    maskdc_in = din("maskdc_in", [128, 256])
    s5init_in = din("s5init_in", [128, 2, 32, 2])
    lruinit_in = din("lruinit_in", [128, 2, KT])
    consts_in = din("consts_in", [128, 128 * 3 + 512 * 2])
    ada_w = din("ada_w", [2, D, 9 * D])
    ada_b = din("ada_b", [128, 2, 72])
    ln_g = din("ln_g", [128, 2, 3, KT])
    ln_b = din("ln_b", [128, 2, 3, KT])
    w_up = din("w_up", [2, 2, D, 2 * DFF])
    w_down = din("w_down", [2, 2, DFF, D])
    s5_w_in = din("s5_w_in", [D, D])
    s5_w_glu = din("s5_w_glu", [D, 2 * D])
    s5_par = din("s5_par", [128, 2, 32, 3])
    s5_bc = din("s5_bc", [128, 4, 32, 16])
    s5_dsk = din("s5_dsk", [128, D])
    lru_w_in = din("lru_w_in", [D, 2 * D])
    lru_w_out = din("lru_w_out", [D, D])
    lru_conv = din("lru_conv", [128, 5, KT])
    lru_gw = din("lru_gw", [128, 4, KT, 128])
    lru_gp = din("lru_gp", [128, 2, 3, KT])

    y_out = dout("y_out", [D, T])
    st5_out = dout("st5_out", [128, 8, 8, 2, 8])
    stl_out = dout("stl_out", [128, 2, KT, 8])

    ctx = []

    def sb(name, shape, dt=F32):
        t = nc.sbuf_tensor(name, list(shape), dt)
        ctx.append(t)
        return t.__enter__()

    def ps(name):
        t = nc.psum_tensor(name, [128, 512], F32)
        ctx.append(t)
        return t.__enter__()

    xs = sb("xs", [128, KT, T])
    hm = sb("hm", [128, KT, T], BF16)
    SCRN = 29900
    scr = sb("scr", [128, SCRN])
    cst = sb("cst", [128, 128 * 3])
    identb = sb("identb", [128, 128], BF16)
    Jb = sb("Jb", [128, 128], BF16)
    onesb = sb("onesb", [128, 128], BF16)
    mkF = sb("mkF", [128, 2, 256], BF16)
    mkB = sb("mkB", [128, 2, 256], BF16)
    modt = sb("modt", [128, 2, 72])
    adab = sb("adab", [128, 2, 72])
    lng = sb("lng", [128, 2, 3, KT])
    lnb = sb("lnb", [128, 2, 3, KT])
    sm = sb("sm", [128, 320])
    condt = sb("condt", [128, KT])
    condb = sb("condb", [128, KT], BF16)
    flags = sb("flags", [128, 4])
    maskdc = sb("maskdc", [128, 256])
    banks = [ps("ps%d" % i) for i in range(8)]
    dummy = sb("dummyt", [128, 8])
    identf = cst[:, 0:128]
    Jf = cst[:, 256:384]

    def carve(off, shape, dt=F32):
        n = int(np.prod(shape))
        words = n if dt == F32 else (n + 1) // 2
        v = scr[:, off:off + words]
        if dt != F32:
            v = v.bitcast(dt)
        if len(shape) > 1:
            names = " ".join("a%d" % i for i in range(len(shape)))
            kw = {"a%d" % i: shape[i] for i in range(1, len(shape))}
            v = v.rearrange("p (%s) -> p %s" % (names, names), **kw)
        return v, off + words

    sm_off = [0]

    def smalloc(n):
        o = sm_off[0]
        sm_off[0] += n
        assert sm_off[0] <= 320
        return sm[:, o:o + n]

    def dma(out, in_, w, r=(), stream=None, slow=False):
        if slow:
            P.add('sp', lambda e, out=out, in_=in_: e.dma_start(out=out, in_=in_, allow_slow_non_contiguous=True), r=r, w=w, stream=stream)
        else:
            P.add('sp', lambda e, out=out, in_=in_: e.dma_start(out=out, in_=in_), r=r, w=w, stream=stream)

    dma(cst[:], consts_in[:, 0:384], [('cst',)], stream='cst')
    mskst, _ = carve(20000, [1024])
    dma(mskst, consts_in[:, 384:1408], [('mskst',)], stream='mskst')
    dma(condt[:], cond_in[:, :], [('cond',)], stream='cond')
    dma(flags[:], flags_in[:, :], [('flags',)], stream='flags')
    dma(maskdc[:], maskdc_in[:, :], [('maskdc',)], stream='maskdc')
    dma(adab[:], ada_b[:, :, :], [('adab',)], stream='adab')
    dma(lng[:], ln_g[:, :, :, :], [('lng',)], stream='lng')
    dma(lnb[:], ln_b[:, :, :, :], [('lnb',)], stream='lnb')
    for kt in range(KT):
        dma(xs[:, kt, :], x_in[kt * 128:(kt + 1) * 128, :], [('xs', kt, b) for b in range(NBLK)], stream='x%d' % (kt % 2))

    P.add('dve', lambda e: e.tensor_copy(out=identb[:], in_=cst[:, 0:128]), r=[('cst',)], w=[('identb',)])
    P.add('dve', lambda e: e.tensor_copy(out=onesb[:], in_=cst[:, 128:256]), r=[('cst',)], w=[('onesb',)])
    P.add('dve', lambda e: e.tensor_copy(out=Jb[:], in_=cst[:, 256:384]), r=[('cst',)], w=[('Jb',)])
    P.add('dve', lambda e: e.tensor_copy(out=mkF[:], in_=mskst[:, 0:512].rearrange("p (a b) -> p a b", a=2)), r=[('mskst',)], w=[('mkF',)])
    P.add('dve', lambda e: e.tensor_copy(out=mkB[:], in_=mskst[:, 512:1024].rearrange("p (a b) -> p a b", a=2)), r=[('mskst',)], w=[('mkB',)])

    pe_w, o_pe = carve(14000, [2, 4])
    Es, o_pe = carve(o_pe, [2, 64])
    Ec, o_pe = carve(o_pe, [2, 64])
    pidx, o_pe = carve(o_pe, [2])
    ptmp, o_pe = carve(o_pe, [2, 4])
    pidf, o_pe = carve(o_pe, [2])
    P.add('pool', lambda e: e.iota(pidf, pattern=[[128, 2]], base=0, channel_multiplier=1, allow_small_or_imprecise_dtypes=True),
          w=[('pidf',)])
    P.add('act', lambda e: e.activation(out=pe_w[:, :, 0], in_=pidf, func=AF.Exp, scale=-math.log(10000.0) / 256.0),
          r=[('pidf',)], w=[('pe_w', 0)])
    P.add('act', lambda e: e.activation(out=pe_w[:, :, 1], in_=pe_w[:, :, 0], func=AF.Sin), r=[('pe_w', 0)], w=[('pe_w', 1)])
    P.add('act', lambda e: e.activation(out=pe_w[:, :, 3], in_=pe_w[:, :, 0], func=AF.Sin, scale=0.5), r=[('pe_w', 0)], w=[('pe_w', 3)])
    P.add('dve', lambda e: e.tensor_tensor(out=pe_w[:, :, 2], in0=pe_w[:, :, 3], in1=pe_w[:, :, 3], op=ALU.mult), r=[('pe_w', 3)], w=[('pe_w', 2)])
    P.add('dve', lambda e: e.tensor_scalar(out=pe_w[:, :, 2], in0=pe_w[:, :, 2], scalar1=-2.0, scalar2=1.0, op0=ALU.mult, op1=ALU.add),
          r=[('pe_w', 2)], w=[('pe_w', 2)])
    P.add('dve', lambda e: e.memset(Es[:, :, 0:1], 0.0), w=[('Es',)])
    P.add('dve', lambda e: e.memset(Ec[:, :, 0:1], 1.0), w=[('Ec',)])
    for n in range(63):
        def stepfn(e, n=n):
            e.tensor_tensor(out=ptmp[:, :, 0], in0=Es[:, :, n], in1=pe_w[:, :, 2], op=ALU.mult)
            e.tensor_tensor(out=ptmp[:, :, 1], in0=Ec[:, :, n], in1=pe_w[:, :, 1], op=ALU.mult)
            e.tensor_tensor(out=ptmp[:, :, 2], in0=Ec[:, :, n], in1=pe_w[:, :, 2], op=ALU.mult)
            e.tensor_tensor(out=ptmp[:, :, 3], in0=Es[:, :, n], in1=pe_w[:, :, 1], op=ALU.mult)
            e.tensor_tensor(out=Es[:, :, n + 1], in0=ptmp[:, :, 0], in1=ptmp[:, :, 1], op=ALU.add)
            return e.tensor_tensor(out=Ec[:, :, n + 1], in0=ptmp[:, :, 2], in1=ptmp[:, :, 3], op=ALU.subtract)
        P.seq('dve', stepfn, r=[('pe_w', 1), ('pe_w', 2), ('Es',), ('Ec',)], w=[('Es',), ('Ec',), ('ptmp',)])
    P.add('dve', lambda e: e.tensor_scalar(out=Es[:], in0=Es[:], scalar1=flags[:, 0:1], scalar2=ALPHA, op0=ALU.mult, op1=ALU.mult),
          r=[('Es',), ('flags',)], w=[('Es',)])
    P.add('dve', lambda e: e.tensor_scalar(out=Ec[:], in0=Ec[:], scalar1=flags[:, 0:1], scalar2=ALPHA, op0=ALU.mult, op1=ALU.mult),
          r=[('Ec',), ('flags',)], w=[('Ec',)])
    for kt in range(KT):
        par = kt % 2
        q = kt // 2
        tab = Es if q % 2 == 0 else Ec
        xv = xs[:, kt, :].rearrange("p (r c) -> p r c", c=64)
        if q < 2:
            ev = tab[:, par, 0:32].unsqueeze(2).broadcast_to([128, 32, 64])
        else:
            ev = tab[:, par, 0:64].unsqueeze(1).broadcast_to([128, 32, 64])
        P.add('dve', lambda e, xv=xv, ev=ev: e.scalar_tensor_tensor(out=xv, in0=xv, scalar=ALPHA, in1=ev, op0=ALU.mult, op1=ALU.add),
              r=[('Es',), ('Ec',)] + [('xs', kt, b) for b in range(NBLK)], w=[('xs', kt, b) for b in range(NBLK)])
    P.add('act', lambda e: e.activation(out=condb[:], in_=condt[:], func=AF.Silu), r=[('cond',)], w=[('condb',)])
    ada_cnt = [0]

    def ada_bufs(width, off):
        ast, o_ = carve(off, [2, KT, width])
        abf, o_ = carve(o_, [2, KT, width], BF16)
        rwt, o_ = carve(o_, [2, width])
        return ast, abf, rwt

    def ada_load(layer, ch, width, off, tag):
        ast, abf, rwt = ada_bufs(width, off)
        sl = ch % 2
        src = ada_w[layer, :, ch * width:(ch + 1) * width].rearrange("(k p) c -> p k c", p=128)
        dma(ast[:, sl], src, [(tag + 'st', sl)], stream=tag + 'st%d' % sl)
        if sl == 0:
            P.add('act', lambda e: e.copy(out=abf[:, sl], in_=ast[:, sl]), r=[(tag + 'st', sl)], w=[(tag + 'bf', sl)])
        else:
            P.add('dve', lambda e: e.tensor_copy(out=abf[:, sl], in_=ast[:, sl]), r=[(tag + 'st', sl)], w=[(tag + 'bf', sl)])

    def ada_compute(layer, ch, width, off, tag):
        ast, abf, rwt = ada_bufs(width, off)
        sl = ch % 2
        nmt = width // 128
        for kt in range(KT):
            P.add('pe', lambda e, kt=kt: e.matmul(banks[6][0:1, 0:width], condb[:, kt:kt + 1], abf[:, sl, kt, :],
                                                   start=(kt == 0), stop=(kt == KT - 1)),
                  r=[(tag + 'bf', sl), ('condb',)], w=[('ps', 6)])
        P.add('act', lambda e: e.copy(out=rwt[0:1, sl, :], in_=banks[6][0:1, 0:width]), r=[('ps', 6)], w=[(tag + 'row', sl)])
        m0 = ch * nmt
        for mm in range(nmt):
            P.add('pe', lambda e, mm=mm: e.matmul(banks[7][:, mm:mm + 1], rwt[0:1, sl, mm * 128:(mm + 1) * 128], cst[0:1, 128:129],
                                                   start=True, stop=True), r=[(tag + 'row', sl), ('cst',)], w=[('ps', 7)])
        P.add('dve', lambda e: e.tensor_tensor(out=modt[:, layer, m0:m0 + nmt], in0=banks[7][:, 0:nmt], in1=adab[:, layer, m0:m0 + nmt], op=ALU.add),
              r=[('ps', 7), ('adab',)], w=[('modt', layer)])

    def ada_chunk(layer, ch, width, off, tag):
        ada_load(layer, ch, width, off, tag)
        ada_compute(layer, ch, width, off, tag)

    for ch in range(18):
        ada_chunk(0, ch, 512, 0, 'adaA')

    def modv(layer, sub, which):
        o = (sub * 3 + which) * 8
        return modt[:, layer, o:o + 8]

    s1 = {}
    shv = {}
    gtv = {}
    for layer in range(2):
        for sub in range(3):
            s1[(layer, sub)] = smalloc(8)
            shv[(layer, sub)] = modv(layer, sub, 0)
            gtv[(layer, sub)] = smalloc(8)

    def derive(layer):
        for sub in range(3):
            a = s1[(layer, sub)]
            P.add('dve', lambda e, a=a, layer=layer, sub=sub: e.tensor_scalar(
                out=a, in0=modv(layer, sub, 1), scalar1=1.0, scalar2=1.0 / ALPHA, op0=ALU.add, op1=ALU.mult),
                r=[('modt', layer)], w=[('sm', 's1', layer, sub)])
            g = gtv[(layer, sub)]
            wgt = 1.0 if sub == 1 else 0.5
            P.add('dve', lambda e, g=g, layer=layer, sub=sub, wgt=wgt: e.tensor_scalar(
                out=g, in0=modv(layer, sub, 2), scalar1=wgt, scalar2=None, op0=ALU.mult),
                r=[('modt', layer)], w=[('sm', 'gt', layer, sub)])
    derive(0)
    lnG = {}
    lnB = {}
    hS = {}
    hB = {}
    order = [(l, s) for l in range(2) for s in range(3)]
    for idx, (layer, sub) in enumerate(order):
        final = (idx == len(order) - 1)
        sc = 1.0 if final else ALPHA
        gg = smalloc(8)
        bb = smalloc(8)
        P.add('dve', lambda e, gg=gg, layer=layer, sub=sub, sc=sc: e.tensor_scalar(
            out=gg, in0=lng[:, layer, sub, :], scalar1=sc, scalar2=None, op0=ALU.mult), r=[('lng',)], w=[('sm', 'lnG', idx)])
        P.add('dve', lambda e, bb=bb, layer=layer, sub=sub, sc=sc: e.tensor_scalar(
            out=bb, in0=lnb[:, layer, sub, :], scalar1=sc, scalar2=None, op0=ALU.mult), r=[('lnb',)], w=[('sm', 'lnB', idx)])
        lnG[idx] = gg
        lnB[idx] = bb
        if not final:
            nl, nsub = order[idx + 1]
            hB[idx] = (nl, nsub)

    for kt in range(KT):
        for b in range(NBLK):
            P.add('act', lambda e, kt=kt, b=b: e.activation(
                out=hm[:, kt, b * BLK:(b + 1) * BLK], in_=xs[:, kt, b * BLK:(b + 1) * BLK], func=AF.Identity,
                scale=s1[(0, 0)][:, kt:kt + 1], bias=shv[(0, 0)][:, kt:kt + 1]),
                r=[('xs', kt, b), ('sm', 's1', 0, 0), ('modt', 0)], w=[('hm', kt, b)])
    scale_ops = []
    for kt in range(KT):
        for b in range(NBLK):
            scale_ops.append(dict(eng='pool', fn=(lambda e, kt=kt, b=b: e.tensor_scalar(
                out=xs[:, kt, b * BLK:(b + 1) * BLK], in0=xs[:, kt, b * BLK:(b + 1) * BLK], scalar1=ALPHA, scalar2=None,
                op0=ALU.mult)), r=[('xs', kt, b)], w=[('xs', kt, b)], stream=None))
    nhm = KT * NBLK

    LNB = 3600

    def ffn(layer, sub, ln_idx, soft=False, hook=None):
        j = sub // 2
        wu = w_up[layer, j]
        wd = w_down[layer, j]
        o = LNB
        hbuf, o = carve(o, [6, T], BF16)
        ust, o = carve(o, [2, KT, 128])
        ubf, o = carve(o, [4, KT, 128], BF16)
        dst_, o = carve(o, [2, 6, 128])
        dbf, o = carve(o, [2, 6, 128], BF16)
        sg, o = carve(o, [2, BLK])
        dall, o = carve(o, [KT, 5, 128], BF16)
        lo = 0
        gt = gtv[(layer, sub)]
        uci = [0]

        def load_up(f, ag):
            sl = uci[0] % 2
            uci[0] += 1
            col = ag * DFF + f * 128
            src = wu[:, col:col + 128].rearrange("(k p) c -> p k c", p=128)
            dma(ust[:, sl], src, [('ust', sl)], stream='ust%d' % sl)
            bs = (f % 2) * 2 + ag
            P.add('pool', lambda e, sl=sl, bs=bs: e.tensor_copy(out=ubf[:, bs], in_=ust[:, sl]),
                  r=[('ust', sl)], w=[('ubf', bs)])

        dci = [0]

        def load_down(f0, nf, m):
            sl = dci[0] % 2
            dci[0] += 1
            src = wd[f0 * 128:(f0 + nf) * 128, m * 128:(m + 1) * 128].rearrange("(f p) c -> p f c", p=128)
            dma(dst_[:, sl, 0:nf], src, [('dst', sl)], stream='dst%d' % sl)
            P.add('pool', lambda e, sl=sl, nf=nf: e.tensor_copy(out=dbf[:, sl, 0:nf], in_=dst_[:, sl, 0:nf]),
                  r=[('dst', sl)], w=[('dbf', sl)])
            return sl

        pcount = [0]
        for (f0, f1) in SECTIONS:
            nf = f1 - f0
            load_up(f0, 0)
            load_up(f0, 1)
            for f in range(f0, f1):
                if f + 1 < f1:
                    load_up(f + 1, 0)
                    load_up(f + 1, 1)
                for b in range(NBLK):
                    pp = pcount[0] % 2
                    pcount[0] += 1
                    pa = banks[pp * 2]
                    pg = banks[pp * 2 + 1]
                    for ag, pt in ((0, pa), (1, pg)):
                        bs = (f % 2) * 2 + ag
                        for kt in range(KT):
                            P.add('pe', lambda e, pt=pt, bs=bs, kt=kt, b=b: e.matmul(
                                pt[:, :], ubf[:, bs, kt, :], hm[:, kt, b * BLK:(b + 1) * BLK],
                                start=(kt == 0), stop=(kt == KT - 1)),
                                r=[('ubf', bs), ('hm', kt, b)], w=[('ps', pp * 2 + ag)])
                    P.add('act', lambda e, pg=pg, pp=pp: e.activation(out=sg[:, pp], in_=pg[:, :], func=AF.Silu),
                          r=[('ps', pp * 2 + 1)], w=[('sg', pp)])
                    P.add('dve', lambda e, pa=pa, pp=pp, f=f, f0=f0, b=b: e.tensor_tensor(
                        out=hbuf[:, f - f0, b * BLK:(b + 1) * BLK], in0=sg[:, pp], in1=pa[:, :], op=ALU.mult),
                        r=[('sg', pp), ('ps', pp * 2)], w=[('h', f - f0, b)])
                if hook is not None:
                    hook(f)
            last = (f1 == NF)
            if not last:
                sl_next = load_down(f0, nf, 0)
                for m in range(KT):
                    sl = sl_next
                    if m + 1 < KT:
                        sl_next = load_down(f0, nf, m + 1)
                    for b in range(NBLK):
                        pb = 4 + (pcount[0] % 2)
                        pcount[0] += 1
                        for fi in range(nf):
                            P.add('pe', lambda e, pb=pb, sl=sl, fi=fi, b=b, nf=nf: e.matmul(
                                banks[pb][:, :], dbf[:, sl, fi, :], hbuf[:, fi, b * BLK:(b + 1) * BLK],
                                start=(fi == 0), stop=(fi == nf - 1)),
                                r=[('dbf', sl), ('h', fi, b)], w=[('ps', pb)])
                        P.add('dve', lambda e, pb=pb, m=m, b=b: e.scalar_tensor_tensor(
                            out=xs[:, m, b * BLK:(b + 1) * BLK], in0=banks[pb][:, :], scalar=gt[:, m:m + 1],
                            in1=xs[:, m, b * BLK:(b + 1) * BLK], op0=ALU.mult, op1=ALU.add),
                            r=[('ps', pb), ('xs', m, b), ('sm', 'gt', layer, sub)], w=[('xs', m, b)])
            else:
                for m in range(KT):
                    sl = dci[0] % 2
                    dci[0] += 1
                    src = wd[f0 * 128:(f0 + nf) * 128, m * 128:(m + 1) * 128].rearrange("(f p) c -> p f c", p=128)
                    dma(dst_[:, sl, 0:nf], src, [('dst', sl)], stream='dst%d' % sl)
                    P.add('pool', lambda e, sl=sl, nf=nf, m=m: e.tensor_copy(out=dall[:, m, 0:nf], in_=dst_[:, sl, 0:nf]),
                          r=[('dst', sl)], w=[('dall', m)])
                for b in range(NBLK + 1):
                    if b < NBLK:
                        for m in range(KT):
                            pb = 4 + (pcount[0] % 2)
                            pcount[0] += 1
                            for fi in range(nf):
                                P.add('pe', lambda e, pb=pb, m=m, fi=fi, b=b, nf=nf: e.matmul(
                                    banks[pb][:, :], dall[:, m, fi, :], hbuf[:, fi, b * BLK:(b + 1) * BLK],
                                    start=(fi == 0), stop=(fi == nf - 1)),
                                    r=[('dall', m), ('h', fi, b)], w=[('ps', pb)])
                            P.add('dve', lambda e, pb=pb, m=m, b=b: e.scalar_tensor_tensor(
                                out=xs[:, m, b * BLK:(b + 1) * BLK], in0=banks[pb][:, :], scalar=gt[:, m:m + 1],
                                in1=xs[:, m, b * BLK:(b + 1) * BLK], op0=ALU.mult, op1=ALU.add),
                                r=[('ps', pb), ('xs', m, b), ('sm', 'gt', layer, sub)], w=[('xs', m, b)])
                    if soft and b == NBLK - 1:
                        P.barrier(dummy[:])
                    if b >= 1:
                        layernorm(b - 1, ln_idx, lo)

    def layernorm(b, idx, lo):
        final = (idx == 5)
        o = lo
        zb, o = carve(o, [2, BLK], BF16)
        zq, o = carve(o, [2, BLK], BF16)
        mean, o = carve(o, [BLK])
        var, o = carve(o, [BLK])
        rstd, o = carve(o, [BLK])
        tt, o = carve(o, [2, BLK])
        cs = slice(b * BLK, (b + 1) * BLK)
        for kt in range(KT):
            sl = kt % 2
            P.add('act', lambda e, kt=kt, sl=sl: e.copy(out=zb[:, sl], in_=xs[:, kt, cs]),
                  r=[('xs', kt, b)], w=[('zb', sl)])
            P.add('act', lambda e, kt=kt, sl=sl: e.activation(out=zq[:, sl], in_=xs[:, kt, cs], func=AF.Square),
                  r=[('xs', kt, b)], w=[('zq', sl)])
            P.add('pe', lambda e, kt=kt, sl=sl: e.matmul(banks[6][:, :], onesb[:, :], zb[:, sl], start=(kt == 0), stop=(kt == KT - 1)),
                  r=[('zb', sl), ('onesb',)], w=[('ps', 6)])
            P.add('pe', lambda e, kt=kt, sl=sl: e.matmul(banks[7][:, :], onesb[:, :], zq[:, sl], start=(kt == 0), stop=(kt == KT - 1)),
                  r=[('zq', sl), ('onesb',)], w=[('ps', 7)])
        P.add('dve', lambda e: e.tensor_scalar(out=mean, in0=banks[6][:, :], scalar1=1.0 / D, scalar2=None, op0=ALU.mult),
              r=[('ps', 6)], w=[('mean',)])
        P.add('dve', lambda e: e.tensor_tensor(out=var, in0=mean, in1=mean, op=ALU.mult), r=[('mean',)], w=[('var',)])
        P.add('dve', lambda e: e.scalar_tensor_tensor(out=var, in0=banks[7][:, :], scalar=1.0 / D, in1=var,
                                                      op0=ALU.mult, op1=ALU.subtract),
              r=[('ps', 7), ('var',)], w=[('var',)])
        P.add('dve', lambda e: e.tensor_scalar(out=var, in0=var, scalar1=LN_EPS, scalar2=None, op0=ALU.add), r=[('var',)], w=[('var',)])
        P.add('act', lambda e: e.activation(out=rstd, in_=var, func=AF.Sqrt), r=[('var',)], w=[('rstd',)])
        P.add('dve', lambda e: e.reciprocal(out=rstd, in_=rstd), r=[('rstd',)], w=[('rstd',)])
        for kt in range(KT):
            sl = kt % 2
            P.add('dve', lambda e, kt=kt, sl=sl: e.tensor_tensor(out=tt[:, sl], in0=xs[:, kt, cs], in1=mean, op=ALU.subtract),
                  r=[('xs', kt, b), ('mean',)], w=[('tt', sl)])
            P.add('dve', lambda e, kt=kt, sl=sl: e.scalar_tensor_tensor(
                out=tt[:, sl], in0=tt[:, sl], scalar=lnG[idx][:, kt:kt + 1], in1=rstd, op0=ALU.mult, op1=ALU.mult),
                r=[('tt', sl), ('rstd',), ('sm', 'lnG', idx)], w=[('tt', sl)])
            P.add('act', lambda e, kt=kt, sl=sl: e.activation(
                out=xs[:, kt, cs], in_=tt[:, sl], func=AF.Identity, bias=lnB[idx][:, kt:kt + 1], scale=1.0),
                r=[('tt', sl), ('sm', 'lnB', idx)], w=[('xs', kt, b)])
            if final:
                dma(y_out[kt * 128:(kt + 1) * 128, cs], xs[:, kt, cs], [('yout', kt, b)], r=[('xs', kt, b)], stream='yo%d' % (kt % 4))
            else:
                nl, nsub = hB[idx]
                P.add('act', lambda e, kt=kt, nl=nl, nsub=nsub: e.activation(
                    out=hm[:, kt, cs], in_=xs[:, kt, cs], func=AF.Identity,
                    scale=s1[(nl, nsub)][:, kt:kt + 1], bias=shv[(nl, nsub)][:, kt:kt + 1]),
                    r=[('xs', kt, b), ('sm', 's1', nl, nsub), ('modt', nl)], w=[('hm', kt, b)])


    class Loader:
        def __init__(self, name, o, nst=2, nbf=4):
            self.name = name
            self.st, o = carve(o, [nst, KT, 128])
            self.bf, o = carve(o, [nbf, KT, 128], BF16)
            self.nst, self.nbf, self.i, self.end = nst, nbf, 0, o

        def load(self, src):
            sl = self.i % self.nst
            bs = self.i % self.nbf
            self.i += 1
            dma(self.st[:, sl], src, [(self.name + 'st', sl)], stream=self.name + 'st%d' % sl)
            P.add('pool', lambda e, sl=sl, bs=bs: e.tensor_copy(out=self.bf[:, bs], in_=self.st[:, sl]),
                  r=[(self.name + 'st', sl)], w=[(self.name + 'bf', bs)])
            return bs

    def wsrc(wap, col):
        return wap[:, col:col + 128].rearrange("(k p) c -> p k c", p=128)

    def s5_mixer():
        layer, sub, ln_idx = 0, 1, 1
        o = 0
        yfm, o = carve(o, [KT, T], BF16)
        ld = Loader('s5w', o, 1, 2)
        o = ld.end
        par, o = carve(o, [2, 32, 3])
        s5i, o = carve(o, [2, 32, 2])
        dskt, o = carve(o, [128])
        utm, o = carve(o, [8, 16, 16])
        U, o = carve(o, [2, 8, 128], BF16)
        SL, o = carve(o, [2, 4, 256])
        SprevZ, o = carve(o, [2, 4, 2, 2, 128], BF16)
        Sprev = SprevZ[:].rearrange("p a b c d e -> p (a b c d e)")[:, 0:2048].rearrange("p (a b c d) -> p a b c d", a=2, b=4, c=2)
        mtb, o = carve(o, [512])
        Kall, o = carve(o, [2, 2, 2, 256], BF16)
        Cma, o = carve(o, [2, 4, 2, 256], BF16)
        Gp, o = carve(o, [2, 2, 256], BF16)
        Hp, o = carve(o, [2, 2, 256], BF16)
        Fm, o = carve(o, [2, 2, 2, 64], BF16)
        t1, o = carve(o, [2, 256])
        Pa2, o = carve(o, [2, 2, 2, 4, 32])
        Pd2, o = carve(o, [2, 2, 2, 4, 32])
        tb, o = carve(o, [15, 2, 32])
        bbp2, o = carve(o, [2, 2, 2, 4, 16])
        bcp2, o = carve(o, [2, 4, 4, 16])
        scn, o = carve(o, [3, 8, 2, 8])
        scn2, o = carve(o, [3, 8, 2])
        Wt, o = carve(o, [2, 2, 4, 16])
        ptw, o = carve(o, [2, 2, 4, 2])
        wtmp, o = carve(o, [2, 4, 8])
        ArX, o = carve(o, [8, 2])
        AiX, o = carve(o, [8, 2])
        A2rX, o = carve(o, [8, 2])
        A2iX, o = carve(o, [8, 2])
        Bnd, o = carve(o, [9, 8, 2])
        CinB, o = carve(o, [8, 2, 8])
        ytm = Sprev[:].rearrange("p a b c d -> p (a b c d)").rearrange("p (t c) -> p t c", t=16)
        ptw2, o = carve(o, [2, 2, 4, 8])
        LNt, o = carve(o, [2, 2, 2, 4])
        assert o <= SCRN, o
        dma(par[:], s5_par[:, :, :, :], [('s5par',)], stream='s5par')
        dma(s5i[:], s5init_in[:, :, :, :], [('s5i',)], stream='s5i')
        TB = lambda n: tb[:, n]
        K = [('s5tb',)]

        def pool(fn, r=K, w=K):
            P.seq('pool', fn, r=r, w=w)

        def tt(e, out, a, b, op):
            return e.tensor_tensor(out=out, in0=a, in1=b, op=op)

        are, aim, ldt = par[:, :, :, 0], par[:, :, :, 1], par[:, :, :, 2]
        P.add('act', lambda e: e.activation(out=TB(0), in_=ldt, func=AF.Exp), r=[('s5par',)], w=K)
        pool(lambda e: tt(e, TB(1), TB(0), are, ALU.mult), r=K + [('s5par',)])
        pool(lambda e: tt(e, TB(2), TB(0), aim, ALU.mult))
        P.add('act', lambda e: e.activation(out=TB(3), in_=TB(1), func=AF.Exp, scale=0.125), r=K, w=K)
        P.add('act', lambda e: e.activation(out=TB(4), in_=TB(2), func=AF.Sin, scale=0.125), r=K, w=K)
        P.add('act', lambda e: e.activation(out=TB(5), in_=TB(2), func=AF.Sin, scale=0.0625), r=K, w=K)
        pool(lambda e: tt(e, TB(5), TB(5), TB(5), ALU.mult))
        pool(lambda e: e.tensor_scalar(out=TB(5), in0=TB(5), scalar1=-2.0, scalar2=1.0, op0=ALU.mult, op1=ALU.add))
        pool(lambda e: tt(e, TB(6), TB(3), TB(5), ALU.mult))
        pool(lambda e: tt(e, TB(7), TB(3), TB(4), ALU.mult))
        for _ in range(3):
            def sq(e):
                tt(e, TB(8), TB(6), TB(6), ALU.mult)
                tt(e, TB(9), TB(7), TB(7), ALU.mult)
                tt(e, TB(10), TB(6), TB(7), ALU.mult)
                tt(e, TB(6), TB(8), TB(9), ALU.subtract)
                return e.tensor_scalar(out=TB(7), in0=TB(10), scalar1=2.0, scalar2=None, op0=ALU.mult)
            pool(sq)
        def qfn(e):
            tt(e, TB(8), are, are, ALU.mult)
            tt(e, TB(9), aim, aim, ALU.mult)
            tt(e, TB(8), TB(8), TB(9), ALU.add)
            e.tensor_scalar(out=TB(9), in0=TB(6), scalar1=-1.0, scalar2=None, op0=ALU.add)
            tt(e, TB(10), TB(9), are, ALU.mult)
            tt(e, TB(11), TB(7), aim, ALU.mult)
            tt(e, TB(10), TB(10), TB(11), ALU.add)
            tt(e, TB(11), TB(7), are, ALU.mult)
            tt(e, TB(12), TB(9), aim, ALU.mult)
            return tt(e, TB(11), TB(11), TB(12), ALU.subtract)
        pool(qfn, r=K + [('s5par',)])
        P.add('dve', lambda e: e.reciprocal(out=TB(8), in_=TB(8)), r=K, w=K)
        pool(lambda e: tt(e, TB(10), TB(10), TB(8), ALU.mult))
        pool(lambda e: tt(e, TB(11), TB(11), TB(8), ALU.mult))
        def invfn(e):
            tt(e, TB(12), TB(6), TB(6), ALU.mult)
            tt(e, TB(13), TB(7), TB(7), ALU.mult)
            return tt(e, TB(12), TB(12), TB(13), ALU.add)
        pool(invfn)
        P.add('dve', lambda e: e.reciprocal(out=TB(12), in_=TB(12)), r=K, w=K)
        pool(lambda e: tt(e, TB(13), TB(6), TB(12), ALU.mult))
        pool(lambda e: e.scalar_tensor_tensor(out=TB(14), in0=TB(7), scalar=-1.0, in1=TB(12), op0=ALU.mult, op1=ALU.mult)
             if False else tt(e, TB(14), TB(7), TB(12), ALU.mult))
        pool(lambda e: e.tensor_scalar(out=TB(14), in0=TB(14), scalar1=-1.0, scalar2=None, op0=ALU.mult))

        bidx = 0
        for eg in range(8):
            g2s = slice(eg * 4, eg * 4 + 4)
            par_ = eg % 2
            KE = [('s5e', par_)]
            Pa, Pd, bbp, bcp = Pa2[:, par_], Pd2[:, par_], bbp2[:, par_], bcp2[:, par_]
            dma(bcp, s5_bc[:, :, g2s, :], [('s5bc', par_)], stream='s5bc%d' % par_)
            bs = ld.load(wsrc(s5_w_in, eg * 128))
            dma(dskt, s5_dsk[:, eg * 128:(eg + 1) * 128], [('dskt',)], stream='dskt')
            def cmulb(e, o_re, o_im, i_re, i_im, s_re, s_im, n):
                m1, m2 = ptw2[:, 0, :, :, 0:n], ptw2[:, 1, :, :, 0:n]
                bcn = lambda ap_: ap_.unsqueeze(3).broadcast_to([128, 2, 4, n])
                tt(e, m1, i_re, bcn(s_re), ALU.mult)
                tt(e, m2, i_im, bcn(s_im), ALU.mult)
                tt(e, o_re, m1, m2, ALU.subtract)
                tt(e, m1, i_re, bcn(s_im), ALU.mult)
                tt(e, m2, i_im, bcn(s_re), ALU.mult)
                tt(e, o_im, m1, m2, ALU.add)

            def powfn(e, g2s=g2s, Pa=Pa, Pd=Pd, bbp=bbp, bcp=bcp):
                e.tensor_copy(out=LNt[:, 0, 0], in_=TB(6)[:, :, g2s])
                e.tensor_copy(out=LNt[:, 1, 0], in_=TB(7)[:, :, g2s])
                e.tensor_copy(out=LNt[:, 0, 1], in_=TB(13)[:, :, g2s])
                e.tensor_copy(out=LNt[:, 1, 1], in_=TB(14)[:, :, g2s])
                for tab, c0 in ((Pa, 15), (Pd, 16)):
                    e.memset(tab[:, 0, :, :, c0], 1.0)
                    e.memset(tab[:, 1, :, :, c0], 0.0)
                w = ptw
                for n in (1, 2, 4, 8):
                    L = (LNt[:, 0, 0], LNt[:, 1, 0])
                    Li = (LNt[:, 0, 1], LNt[:, 1, 1])
                    cmulb(e, Pa[:, 0, :, :, 15 + n:15 + 2 * n], Pa[:, 1, :, :, 15 + n:15 + 2 * n], Pa[:, 0, :, :, 15:15 + n], Pa[:, 1, :, :, 15:15 + n], L[0], L[1], n)
                    cmulb(e, Pa[:, 0, :, :, 16 - 2 * n:16 - n], Pa[:, 1, :, :, 16 - 2 * n:16 - n], Pa[:, 0, :, :, 16 - n:16], Pa[:, 1, :, :, 16 - n:16], Li[0], Li[1], n)
                    cmulb(e, Pd[:, 0, :, :, 17 - 2 * n:17 - n], Pd[:, 1, :, :, 17 - 2 * n:17 - n], Pd[:, 0, :, :, 17 - n:17], Pd[:, 1, :, :, 17 - n:17], L[0], L[1], n)
                    cmulb(e, Pd[:, 0, :, :, 16 + n:16 + 2 * n], Pd[:, 1, :, :, 16 + n:16 + 2 * n], Pd[:, 0, :, :, 16:16 + n], Pd[:, 1, :, :, 16:16 + n], Li[0], Li[1], n)
                    for k in range(2):
                        xr, xi = LNt[:, 0, k], LNt[:, 1, k]
                        a0, a1, a2 = [w[:, i // 2, i % 2].rearrange("p a b -> p b a") for i in range(3)]
                        tt(e, a0, xr, xr, ALU.mult)
                        tt(e, a1, xi, xi, ALU.mult)
                        tt(e, a2, xr, xi, ALU.mult)
                        tt(e, xr, a0, a1, ALU.subtract)
                        e.tensor_scalar(out=xi, in0=a2, scalar1=2.0, scalar2=None, op0=ALU.mult)
                for ri in range(2):
                    e.tensor_copy(out=Pa[:, ri, :, :, 31], in_=LNt[:, ri, 0])
                    e.tensor_copy(out=Pd[:, ri, :, :, 0], in_=LNt[:, ri, 0])
                qr = TB(10)[:, :, g2s].unsqueeze(3).broadcast_to([128, 2, 4, 16])
                qi = TB(11)[:, :, g2s].unsqueeze(3).broadcast_to([128, 2, 4, 16])
                br = bcp[:, 0, :, :].unsqueeze(1).broadcast_to([128, 2, 4, 16])
                bi = bcp[:, 1, :, :].unsqueeze(1).broadcast_to([128, 2, 4, 16])
                x0 = ptw2[:, 0]
                x1 = ptw2[:, 1]
                for half in range(2):
                    hs = slice(half * 8, half * 8 + 8)
                    tt(e, x0, qr[:, :, :, hs], br[:, :, :, hs], ALU.mult)
                    tt(e, x1, qi[:, :, :, hs], bi[:, :, :, hs], ALU.mult)
                    tt(e, bbp[:, 0, :, :, hs], x0, x1, ALU.subtract)
                    tt(e, x0, qr[:, :, :, hs], bi[:, :, :, hs], ALU.mult)
                    tt(e, x1, qi[:, :, :, hs], br[:, :, :, hs], ALU.mult)
                    tt(e, bbp[:, 1, :, :, hs], x0, x1, ALU.add)
            P.seq('pool', powfn, r=K + [('s5bc', par_)], w=KE + [('ptw',)])
            for k4 in range(4):
                bk = banks[4 + (k4 % 2)]
                for kk in range(4):
                    k = k4 * 4 + kk
                    for kt in range(KT):
                        P.add('pe', lambda e, bk=bk, kk=kk, k=k, kt=kt, bs=bs: e.matmul(
                            bk[:, kk * 128:(kk + 1) * 128], hm[:, kt, k::16], ld.bf[:, bs, kt, :],
                            start=(kt == 0), stop=(kt == KT - 1)),
                            r=[('s5wbf', bs)] + [('hm', kt, b) for b in range(NBLK)], w=[('ps', 4 + (k4 % 2))])
                P.add('act', lambda e, bk=bk, k4=k4: e.copy(out=utm[:, :, k4 * 4:(k4 + 1) * 4, :].rearrange("p g k i -> p k g i"), in_=bk[:, :].rearrange("p (k g i) -> p k g i", k=4, g=8)),
                      r=[('ps', 4 + (k4 % 2))], w=[('utm',)])
            ubf = Sprev[:].rearrange("p a b c d -> p (a b c d)").rearrange("p (g x) -> p g x", g=8)
            P.add('act', lambda e: e.copy(out=ubf, in_=utm[:].rearrange("p g k i -> p g (k i)")), r=[('utm',)], w=[('Sprev',)])
            for gl in range(8):
                for kh in range(2):
                    qd = (gl * 2 + kh) % 4
                    P.add('pe', lambda e, gl=gl, kh=kh, qd=qd: e.matmul(
                        banks[6][:, qd * 128:(qd + 1) * 128], ubf[:, gl, kh * 128:(kh + 1) * 128], identb[:, :],
                        start=True, stop=True), r=[('Sprev',), ('identb',)], w=[('ps', 6, qd)])
                    P.add('act', lambda e, gl=gl, kh=kh, qd=qd: e.copy(out=U[:, kh, gl, :], in_=banks[6][:, qd * 128:(qd + 1) * 128]),
                          r=[('ps', 6, qd)], w=[('U',)])
            P.add('pool', lambda e: e.memset(SprevZ[:], 0.0), r=[], w=[('Sprev',)])
            P.add('dve', lambda e: e.tensor_tensor(out=utm[:], in0=utm[:], in1=dskt.rearrange("p (g i) -> p g i", g=8).unsqueeze(2).broadcast_to([128, 8, 16, 16]), op=ALU.mult),
                  r=[('utm',), ('dskt',), ('U',)], w=[('utm',)])

            def plane(e, out, Ptab, sl0, vr, vi, sign_im, rows=slice(0, 128), neg=False, d=0, g2l=0):
                Pr = Ptab[rows, 0, d, g2l, sl0:sl0 + 16].unsqueeze(2).broadcast_to([rows.stop - rows.start, 16, 16])
                Pi = Ptab[rows, 1, d, g2l, sl0:sl0 + 16].unsqueeze(2).broadcast_to([rows.stop - rows.start, 16, 16])
                n = rows.stop - rows.start
                vrb = vr.unsqueeze(1).broadcast_to([n, 16, 16])
                vib = vi.unsqueeze(1).broadcast_to([n, 16, 16])
                x0 = t1[rows, 0].rearrange("p (a b) -> p a b", a=16)
                x1 = t1[rows, 1].rearrange("p (a b) -> p a b", a=16)
                o_re = out[0].rearrange("p (a b) -> p a b", a=16)
                o_im = out[1].rearrange("p (a b) -> p a b", a=16)
                tt(e, x0, Pr, vrb, ALU.mult)
                tt(e, x1, Pi, vib, ALU.mult)
                tt(e, o_re, x0, x1, ALU.subtract)
                tt(e, x0, Pr, vib, ALU.mult)
                tt(e, x1, Pi, vrb, ALU.mult)
                if neg:
                    return e.scalar_tensor_tensor(out=o_im, in0=x0, scalar=-1.0, in1=x1, op0=ALU.mult, op1=ALU.subtract)
                return tt(e, o_im, x0, x1, ALU.add)

            def tabs(d):
                tabG, offG = (Pd, 1) if d == 0 else (Pa, 15)
                tabH, offH = (Pa, 0) if d == 0 else (Pd, 16)
                tabC, offC = (Pa, 16) if d == 0 else (Pd, 0)
                return tabG, offG, tabH, offH, tabC, offC

            for d in range(2):
                for g2l in range(4):
                    slot = bidx % 2
                    bidx += 1
                    tabG, offG, tabH, offH, tabC, offC = tabs(d)

                    def genA(e, d=d, g2l=g2l, slot=slot, tabG=tabG, offG=offG):
                        return plane(e, (Gp[:, slot, 0], Gp[:, slot, 1]), tabG, offG, bbp[:, 0, d, g2l, :], bbp[:, 1, d, g2l, :], 1, d=d, g2l=g2l)
                    P.seq('dve', genA, r=KE, w=[('Gp', slot), ('t1',)], nosync=True)
                    for gp in range(2):
                        gl = g2l * 2 + gp
                        rows = slice(gp * 64, gp * 64 + 64)
                        fb = 2 + gp
                        for kh in range(2):
                            for ri in range(2):
                                P.add('pe', lambda e, fb=fb, kh=kh, ri=ri, rows=rows, slot=slot: e.matmul(
                                    banks[fb][:, (kh * 2 + ri) * 64:(kh * 2 + ri + 1) * 64], Gp[rows, slot, ri, kh * 128:(kh + 1) * 128],
                                    identb[rows, rows], start=True, stop=True), r=[('Gp', slot), ('identb',)], w=[('ps', fb)])
                        fs = gp
                        P.add('act', lambda e, fb=fb, fs=fs: e.copy(out=Fm[:, fs].rearrange("p a b c -> p (a b c)"), in_=banks[fb][:, 0:256]),
                              r=[('ps', fb)], w=[('Fm', fs)])
                        for ri in range(2):
                            for kh in range(2):
                                P.add('pe', lambda e, ri=ri, kh=kh, rows=rows, fs=fs, gl=gl, d=d: e.matmul(
                                    banks[7][rows, (d * 2 + ri) * 128:(d * 2 + ri + 1) * 128], Fm[:, fs, kh, ri, :], U[:, kh, gl, :],
                                    start=(kh == 0), stop=(kh == 1)), r=[('Fm', fs), ('U',)], w=[('ps', 7, d)])
                    for ri in range(2):
                        P.add('act', lambda e, ri=ri, d=d, g2l=g2l: e.copy(
                            out=SL[:, ri, g2l, d * 128:(d + 1) * 128], in_=banks[7][:, (d * 2 + ri) * 128:(d * 2 + ri + 1) * 128]),
                            r=[('ps', 7, d)], w=[('SL',)])
            def cap(base, off, dims):
                if not hasattr(base, 'tensor'):
                    base = base[:]
                return bass.AP(tensor=base.tensor, offset=base.offset + off, ap=[list(base.ap[0])] + [list(x) for x in dims])

            def cmulblk(e, o_re, o_im, i_re, i_im, s_re, s_im, tmp):
                m1, m2 = tmp
                tt(e, m1, i_re, s_re, ALU.mult)
                tt(e, m2, i_im, s_im, ALU.mult)
                tt(e, o_re, m1, m2, ALU.subtract)
                tt(e, m1, i_re, s_im, ALU.mult)
                tt(e, m2, i_im, s_re, ALU.mult)
                tt(e, o_im, m1, m2, ALU.add)

            def wfn(e):
                for ri in range(2):
                    e.tensor_copy(out=Wt[:, ri, 0, :, 0], in_=Pa[:, ri, 0, :, 31])
                    e.tensor_copy(out=Wt[:, ri, 1, :, 15], in_=Pa[:, ri, 1, :, 31])
                for n in (1, 2, 4, 8):
                    tmpv = (wtmp[:, 0, :, 0:n], wtmp[:, 1, :, 0:n])
                    bc = lambda ap_: ap_.unsqueeze(2).broadcast_to([128, 4, n])
                    cmulblk(e, Wt[:, 0, 0, :, n:2 * n], Wt[:, 1, 0, :, n:2 * n], Wt[:, 0, 0, :, 0:n], Wt[:, 1, 0, :, 0:n],
                            bc(Wt[:, 0, 0, :, n - 1]), bc(Wt[:, 1, 0, :, n - 1]), tmpv)
                    cmulblk(e, Wt[:, 0, 1, :, 16 - 2 * n:16 - n], Wt[:, 1, 1, :, 16 - 2 * n:16 - n], Wt[:, 0, 1, :, 16 - n:16], Wt[:, 1, 1, :, 16 - n:16],
                            bc(Wt[:, 0, 1, :, 16 - n]), bc(Wt[:, 1, 1, :, 16 - n]), tmpv)
                for ri_t, dst16, dst256 in ((0, ArX, A2rX), (1, AiX, A2iX)):
                    for ri in range(2):
                        e.tensor_copy(out=dst16[:, ri * 4:(ri + 1) * 4, :], in_=Pa[:, ri_t, :, :, 31].rearrange("p d g -> p g d"))
                        e.tensor_copy(out=dst256[:, ri * 4:(ri + 1) * 4, 0], in_=Wt[:, ri_t, 0, :, 15])
                        e.tensor_copy(out=dst256[:, ri * 4:(ri + 1) * 4, 1], in_=Wt[:, ri_t, 1, :, 0])
                for ri in range(2):
                    e.tensor_copy(out=Bnd[:, 0, ri * 4:(ri + 1) * 4, :], in_=s5i[:, ri, g2s, :])
            P.seq('pool', wfn, r=KE + [('s5i',)], w=[('Wt',)])

            def scanfn(e):
                T1, T2, T3 = scn[:, 0], scn[:, 1], scn[:, 2]
                bcb = lambda t: cap(t, 0, [[2, 8], [1, 2], [0, 8]])
                for n in range(1, 16):
                    cur = cap(SL, n, [[256, 8], [143 - 2 * n, 2], [16, 8]])
                    prev = cap(SL, n - 1, [[256, 8], [145 - 2 * n, 2], [16, 8]])
                    tt(e, T1, prev, bcb(ArX), ALU.mult)
                    tt(e, T2, prev, bcb(AiX), ALU.mult)
                    tt(e, T3[:, 0:4], T1[:, 0:4], T2[:, 4:8], ALU.subtract)
                    tt(e, T3[:, 4:8], T1[:, 4:8], T2[:, 0:4], ALU.add)
                    tt(e, cur, cur, T3, ALU.add)
                C1, C2, C3 = scn2[:, 0], scn2[:, 1], scn2[:, 2]
                for n in range(8):
                    send = cap(SL, 16 * n + 15, [[256, 8], [225 - 32 * n, 2]])
                    mkv = cap(maskdc, 16 * n, [[0, 8], [255 - 32 * n, 2]])
                    cin = cap(CinB, n, [[16, 8], [15 - 2 * n, 2]])
                    tt(e, cin, Bnd[:, n], mkv, ALU.mult)
                    tt(e, C1, cin, A2rX[:], ALU.mult)
                    tt(e, C2, cin, A2iX[:], ALU.mult)
                    tt(e, C3[:, 0:4], C1[:, 0:4], C2[:, 4:8], ALU.subtract)
                    tt(e, C3[:, 4:8], C1[:, 4:8], C2[:, 0:4], ALU.add)
                    tt(e, Bnd[:, n + 1], send, C3, ALU.add)
                mt = mtb.rearrange("p (g b j) -> p g b j", g=4, b=8)
                for d in range(2):
                    Sv = [cap(SL, ri * 1024 + d * 128, [[256, 4], [16, 8], [1, 16]]) for ri in range(2)]
                    Wv = [cap(Wt, ri * 128 + d * 64, [[16, 4], [0, 8], [1, 16]]) for ri in range(2)]
                    Cv = [cap(CinB, ri * 64 + d * 8, [[16, 4], [1, 8], [0, 16]]) for ri in range(2)]
                    tt(e, mt, Wv[0], Cv[0], ALU.mult)
                    tt(e, Sv[0], Sv[0], mt, ALU.add)
                    tt(e, mt, Wv[1], Cv[1], ALU.mult)
                    tt(e, Sv[0], Sv[0], mt, ALU.subtract)
                    tt(e, mt, Wv[0], Cv[1], ALU.mult)
                    tt(e, Sv[1], Sv[1], mt, ALU.add)
                    tt(e, mt, Wv[1], Cv[0], ALU.mult)
                    tt(e, Sv[1], Sv[1], mt, ALU.add)
                for d in range(2):
                    for gp in range(2):
                        rows = slice(gp * 64, gp * 64 + 64)
                        Z = SprevZ[rows, :, :, d, gp, :].rearrange("p a b c -> p (a b) c").rearrange("p r (b j) -> p r b j", j=16)
                        Sd = SL[rows, :, :, d * 128:(d + 1) * 128].rearrange("p a b c -> p (a b) c").rearrange("p r (b j) -> p r b j", j=16)
                        if d == 0:
                            e.tensor_copy(out=Z[:, :, :, 1:16], in_=Sd[:, :, :, 0:15])
                            e.tensor_copy(out=Z[:, :, :, 0], in_=CinB[rows, :, 0, :])
                        else:
                            e.tensor_copy(out=Z[:, :, :, 0:15], in_=Sd[:, :, :, 1:16])
                            e.tensor_copy(out=Z[:, :, :, 15], in_=CinB[rows, :, 1, :])
            n_scan0 = len(P.ops)
            P.seq('dve', scanfn, r=[('s5i',), ('maskdc',), ('Wt',)] + KE, w=[('SL',), ('scn',), ('Sprev',), ('mtb',)])
            scan_ops = P.ops[n_scan0:]
            del P.ops[n_scan0:]
            n_b10 = len(P.ops)
            for g2l in range(4):
                yb = 2 + g2l
                for d in range(2):
                    mk = mkF if d == 0 else mkB
                    slot = bidx % 2
                    bidx += 1
                    tabG, offG, tabH, offH, tabC, offC = tabs(d)

                    def genB(e, d=d, g2l=g2l, slot=slot, tabG=tabG, offG=offG, tabH=tabH, offH=offH, tabC=tabC, offC=offC):
                        plane(e, (Gp[:, slot, 0], Gp[:, slot, 1]), tabG, offG, bbp[:, 0, d, g2l, :], bbp[:, 1, d, g2l, :], 1, d=d, g2l=g2l)
                        plane(e, (Hp[:, slot, 0], Hp[:, slot, 1]), tabH, offH, bcp[:, 2, g2l, :], bcp[:, 3, g2l, :], -1, neg=True, d=d, g2l=g2l)
                        plane(e, (Cma[:, d, g2l, 0], Cma[:, d, g2l, 1]), tabC, offC, bcp[:, 2, g2l, :], bcp[:, 3, g2l, :], -1, neg=True, d=d, g2l=g2l)
                    P.seq('dve', genB, r=KE + [('s5bc', par_)], w=[('Gp', slot), ('Hp', slot), ('Cma', d, g2l), ('t1',)], nosync=True)
                    for gp in range(2):
                        gl = g2l * 2 + gp
                        rows = slice(gp * 64, gp * 64 + 64)
                        kb = 0 + gp
                        for kh in range(2):
                            for ri in range(2):
                                P.add('pe', lambda e, kb=kb, kh=kh, ri=ri, rows=rows, slot=slot: e.matmul(
                                    banks[kb][:, kh * 256:(kh + 1) * 256], Gp[rows, slot, ri, kh * 128:(kh + 1) * 128], Hp[rows, slot, ri, :],
                                    start=(ri == 0), stop=(ri == 1)), r=[('Gp', slot), ('Hp', slot)], w=[('ps', kb)])
                        P.add('dve', lambda e, kb=kb, d=d, gp=gp, mk=mk: e.tensor_tensor(
                            out=Kall[:, d, gp], in0=banks[kb][:, :].rearrange("p (a b) -> p a b", a=2), in1=mk[:], op=ALU.mult),
                            r=[('ps', kb), ('mkF',), ('mkB',)], w=[('Kall', d, gp)])
                        for kh in range(2):
                            first = (d == 0 and gp == 0 and kh == 0)
                            P.add('pe', lambda e, yb=yb, d=d, kh=kh, gl=gl, gp=gp, first=first: e.matmul(
                                banks[yb][:, gp * 256:(gp + 1) * 256], U[:, kh, gl, :], Kall[:, d, gp, kh, :], start=first, stop=False,
                                skip_group_check=True), r=[('U',), ('Kall', d, gp)], w=[('ps', yb)])
            b1_ops = P.ops[n_b10:]
            del P.ops[n_b10:]
            merged = []
            si_ = 0
            for op_ in b1_ops:
                merged.append(op_)
                if op_['eng'] == 'dve' and si_ < len(scan_ops):
                    merged.append(scan_ops[si_])
                    si_ += 1
            merged.extend(scan_ops[si_:])
            P.ops.extend(merged)
            stg = scn[:, 0]

            def stgfn(e):
                e.copy(out=stg[:, :, 0, :], in_=cap(SL, 15, [[256, 8], [16, 8]]))
                e.copy(out=stg[:, :, 1, :], in_=cap(SL, 128, [[256, 8], [16, 8]]))
            P.seq('act', stgfn, r=[('SL',)], w=[('scn',)])
            dma(st5_out[:, eg], stg, [('st5', eg)], r=[('scn',)], stream='st5')
            for g2l in range(4):
                yb = 2 + g2l
                for gp in range(2):
                    gl = g2l * 2 + gp
                    cnt = 0
                    for d in range(2):
                        for ri in range(2):
                            P.add('pe', lambda e, yb=yb, d=d, ri=ri, gp=gp, g2l=g2l, cnt=cnt: e.matmul(
                                banks[yb][:, gp * 256:(gp + 1) * 256], SprevZ[:, ri, g2l, d, gp, :], Cma[:, d, g2l, ri, :], start=False, stop=(cnt == 3),
                                skip_group_check=True), r=[('Sprev',), ('Cma', d, g2l)], w=[('ps', yb)])
                            cnt += 1
                for gp in range(2):
                    gl = g2l * 2 + gp
                    P.add('dve', lambda e, yb=yb, gl=gl, gp=gp: e.tensor_tensor(
                        out=utm[:, gl], in0=banks[yb][:, gp * 256:(gp + 1) * 256].rearrange("p (a b) -> p a b", a=16),
                        in1=utm[:, gl], op=ALU.add), r=[('ps', yb), ('utm',)], w=[('utm',)])
            gtmp = SL[:].rearrange("p a b c -> p (a b c)")
            utf = utm[:].rearrange("p g k i -> p (g k i)")
            gtmp4 = gtmp.rearrange("p (g t j) -> p g t j", g=8, t=16)
            P.add('act', lambda e: e.activation(out=gtmp, in_=utf, func=AF.Square), r=[('utm',), ('SL',), ('Sprev',)], w=[('SL',)])
            P.add('dve', lambda e: e.tensor_scalar(out=gtmp, in0=gtmp, scalar1=0.044715, scalar2=1.0, op0=ALU.mult, op1=ALU.add), r=[('SL',)], w=[('SL',)])
            P.add('dve', lambda e: e.tensor_tensor(out=gtmp, in0=gtmp, in1=utf, op=ALU.mult), r=[('SL',), ('utm',)], w=[('SL',)])
            P.add('act', lambda e: e.activation(out=gtmp, in_=gtmp, func=AF.Sigmoid, scale=1.5957691216), r=[('SL',)], w=[('SL',)])
            P.add('dve', lambda e: e.tensor_tensor(out=ytm[:].rearrange("p t (g j) -> p g t j", g=8), in0=gtmp4, in1=utm[:], op=ALU.mult), r=[('SL',), ('utm',)], w=[('Sprev',)])
            for t4 in range(4):
                for tq in range(4):
                    tau = t4 * 4 + tq
                    P.add('pe', lambda e, tq=tq, tau=tau: e.matmul(banks[6][:, tq * 128:(tq + 1) * 128], ytm[:, tau, :], identb[:, :],
                                                                    start=True, stop=True), r=[('Sprev',), ('identb',)], w=[('ps', 6, tq)])
                P.add('act', lambda e, t4=t4, eg=eg: e.copy(
                    out=yfm[:, eg, :].rearrange("p (c t) -> p t c", t=16)[:, t4 * 4:(t4 + 1) * 4, :],
                    in_=banks[6][:, :].rearrange("p (a b) -> p a b", a=4)),
                    r=[('ps', 6, q) for q in range(4)], w=[('yfm', eg)])
        gt = gtv[(layer, sub)]
        sgt, o2 = carve(ld.end, [2, BLK])
        mvt, o2 = carve(o2, [2, BLK])
        pc = 0
        for m in range(KT):
            ba_ = ld.load(wsrc(s5_w_glu, m * 128))
            bg_ = ld.load(wsrc(s5_w_glu, D + m * 128))
            for b in range(NBLK):
                pp = pc % 2
                pc += 1
                for (bs, bank) in ((ba_, pp * 2), (bg_, pp * 2 + 1)):
                    for kt in range(KT):
                        P.add('pe', lambda e, bs=bs, bank=bank, kt=kt, b=b: e.matmul(
                            banks[bank][:, :], ld.bf[:, bs, kt, :], yfm[:, kt, b * BLK:(b + 1) * BLK], start=(kt == 0), stop=(kt == KT - 1)),
                            r=[('s5wbf', bs), ('yfm', kt)], w=[('ps', bank)])
                P.add('act', lambda e, pp=pp: e.activation(out=sgt[:, pp], in_=banks[pp * 2 + 1][:, :], func=AF.Sigmoid),
                      r=[('ps', pp * 2 + 1)], w=[('sgt', pp)])
                P.add('dve', lambda e, pp=pp: e.tensor_tensor(out=mvt[:, pp], in0=banks[pp * 2][:, :], in1=sgt[:, pp], op=ALU.mult),
                      r=[('ps', pp * 2), ('sgt', pp)], w=[('mvt', pp)])
                P.add('dve', lambda e, pp=pp, m=m, b=b: e.scalar_tensor_tensor(
                    out=xs[:, m, b * BLK:(b + 1) * BLK], in0=mvt[:, pp], scalar=gt[:, m:m + 1], in1=xs[:, m, b * BLK:(b + 1) * BLK],
                    op0=ALU.mult, op1=ALU.add), r=[('mvt', pp), ('xs', m, b), ('sm', 'gt', layer, sub)], w=[('xs', m, b)])
        for b in range(NBLK):
            layernorm(b, ln_idx, o2)


    def lru_mixer():
        layer, sub, ln_idx = 1, 1, 4
        o = LNB
        yfm, o = carve(o, [KT, T], BF16)
        ld = Loader('lruw', o, 1, 2)
        o = ld.end
        gwb, o = carve(o, [4, KT, 128], BF16)
        cv, o = carve(o, [5, KT])
        cvb, o = carve(o, [4, KT])
        gpp, o = carve(o, [2, 3, KT])
        li, o = carve(o, [2, KT])
        sc8, o = carve(o, [2, 2, KT])
        Bf = []
        for i in range(5):
            bfi, o = carve(o, [T])
            Bf.append(bfi)
        xcb, o = carve(o, [T], BF16)
        Tt, o = carve(o, [2, 4, 128], BF16)
        lob, o = carve(o, [512], BF16)
        gg, o = carve(o, [2, BLK])
        assert o <= SCRN, o
        dma(cv[:], lru_conv[:, :, :], [('cv',)], stream='cv')
        dma(gpp[:], lru_gp[:, :, :, :], [('gpp',)], stream='gpp')
        dma(li[:], lruinit_in[:, :, :], [('li',)], stream='li')
        for i in range(4):
            dma(ld.st[:, 0], lru_gw[:, i, :, :], [('lruwst', 0)], stream='lruwst0')
            P.add('pool', lambda e, i=i: e.tensor_copy(out=gwb[:, i], in_=ld.st[:, 0]), r=[('lruwst', 0)], w=[('gwb',)])
        P.add('dve', lambda e: e.tensor_scalar(out=cvb[:], in0=cv[:, 0:4, :], scalar1=flags[:, 1:2], scalar2=None, op0=ALU.mult),
              r=[('cv',), ('flags',)], w=[('cvb',)])
        lamv = gpp[:, :, 2, :]
        P.add('act', lambda e: e.activation(out=sc8[:, :, 0, :], in_=lamv, func=AF.Exp, scale=-1.0), r=[('gpp',)], w=[('sc8',)])
        P.add('act', lambda e: e.activation(out=sc8[:, :, 0, :], in_=sc8[:, :, 0, :], func=AF.Ln, bias=1.0), r=[('sc8',)], w=[('sc8',)])
        P.add('dve', lambda e: e.tensor_scalar(out=sc8[:, :, 1, :], in0=sc8[:, :, 0, :], scalar1=-16.0, scalar2=None, op0=ALU.mult), r=[('sc8',)], w=[('sc8',)])
        P.add('dve', lambda e: e.tensor_scalar(out=sc8[:, :, 0, :], in0=sc8[:, :, 0, :], scalar1=-8.0, scalar2=None, op0=ALU.mult), r=[('sc8',)], w=[('sc8',)])

        def sv(buf):
            return buf.rearrange("p (s t) -> p s t", t=256)

        def reverse(src, dst, skey, dkey):
            P.add('dve', lambda e: e.tensor_copy(out=xcb, in_=src), r=[skey], w=[('xcb',)])
            for q in range(4):
                qs = slice(q * 512, (q + 1) * 512)
                P.add('dve', lambda e, qs=qs: e.tensor_tensor(out=lob, in0=src[:, qs], in1=xcb[:, qs], op=ALU.subtract),
                      r=[skey, ('xcb',)], w=[('lob',)])
                for hl, (sbuf_, bank) in enumerate(((xcb, 6), (lob, 7))):
                    for i in range(4):
                        col = (q * 512 + i * 128) if hl == 0 else i * 128
                        P.add('pe', lambda e, sbuf_=sbuf_, bank=bank, i=i, col=col: e.matmul(
                            banks[bank][:, i * 128:(i + 1) * 128], sbuf_[:, col:col + 128], identb[:, :], start=True, stop=True),
                            r=[('xcb',), ('lob',), ('identb',)], w=[('ps', bank)])
                    P.add('act', lambda e, hl=hl, bank=bank: e.copy(out=Tt[:, hl], in_=banks[bank][:, :].rearrange("p (a b) -> p a b", a=4)),
                          r=[('ps', bank)], w=[('Tt', hl)])
                for i in range(4):
                    for hl in range(2):
                        P.add('pe', lambda e, i=i, hl=hl: e.matmul(banks[5][:, (3 - i) * 128:(4 - i) * 128], Tt[:, hl, i, :], Jb[:, :],
                                                                  start=(hl == 0), stop=(hl == 1)),
                              r=[('Tt', hl), ('Jb',)], w=[('ps', 5)])
                P.add('act', lambda e, q=q: e.copy(out=dst[:, (12 - 4 * q) * 128:(16 - 4 * q) * 128], in_=banks[5][:, :]),
                      r=[('ps', 5)], w=[dkey])

        def lru_kt(kt):
            B0, B1, B2, B3, B4 = Bf
            kB = [('B', i) for i in range(5)]
            bx_ = ld.load(wsrc(lru_w_in, kt * 128))
            w = lambda k: cv[:, k, kt:kt + 1]
            wb = lambda k: cvb[:, k, kt:kt + 1]
            for b in range(NBLK):
                bank = b % 2
                cs = slice(b * BLK, (b + 1) * BLK)
                for k in range(KT):
                    P.add('pe', lambda e, bank=bank, k=k, cs=cs, bx_=bx_: e.matmul(banks[bank][:, :], ld.bf[:, bx_, k, :], hm[:, k, cs],
                                                                                 start=(k == 0), stop=(k == KT - 1)),
                          r=[('lruwbf', bx_), ('hm', k, b)], w=[('ps', bank)])
                P.add('act', lambda e, bank=bank, cs=cs: e.activation(out=B1[:, cs], in_=banks[bank][:, :], func=AF.Identity, scale=w(2), bias=w(4)),
                      r=[('ps', bank), ('cv',)], w=[kB[1]])
                P.add('dve', lambda e, bank=bank, cs=cs: e.tensor_copy(out=B0[:, cs], in_=banks[bank][:, :]), r=[('ps', bank)], w=[kB[0]])
            bg_ = ld.load(wsrc(lru_w_in, D + kt * 128))

            def convfn(e):
                xr, xc = sv(B0), sv(B1)
                stt = lambda out, in0, sc: e.scalar_tensor_tensor(out=out, in0=in0, scalar=sc, in1=out, op0=ALU.mult, op1=ALU.add)
                stt(xc[:, :, 2:256], xr[:, :, 0:254], w(0))
                stt(xc[:, :, 1:256], xr[:, :, 0:255], w(1))
                stt(xc[:, :, 0:255], xr[:, :, 1:256], w(3))
                stt(xc[:, 1:8, 0:1], xr[:, 0:7, 254:255], wb(0))
                stt(xc[:, 1:8, 0:1], xr[:, 0:7, 255:256], wb(1))
                stt(xc[:, 1:8, 1:2], xr[:, 0:7, 255:256], wb(0))
                return stt(xc[:, 0:7, 255:256], xr[:, 1:8, 0:1], wb(3))
            P.seq('dve', convfn, r=[kB[0], ('cv',), ('cvb',)], w=[kB[1]])
            def emit_dir(d):
                src, skey = (B1, kB[1]) if d == 0 else (B2, kB[2])
                R, rkey = (B0, kB[0]) if d == 0 else (B1, kB[1])
                GI, A = B3, B4
                if d == 1:
                    P.add('dve', lambda e, src=src: e.tensor_copy(out=xcb, in_=src), r=[skey], w=[('xcb',)])
                for b in range(NBLK):
                    cs = slice(b * BLK, (b + 1) * BLK)
                    for gi_, (dstb, dkey) in enumerate(((R, rkey), (GI, kB[3]))):
                        bank = 2 + gi_
                        P.add('pe', lambda e, bank=bank, d=d, gi_=gi_, cs=cs: e.matmul(banks[bank][:, :], gwb[:, d * 2 + gi_, kt, :], xcb[:, cs],
                                                                                       start=True, stop=True),
                              r=[('gwb',), ('xcb',)], w=[('ps', bank)])
                        P.add('act', lambda e, bank=bank, d=d, gi_=gi_, cs=cs, dstb=dstb: e.activation(
                            out=dstb[:, cs], in_=banks[bank][:, :], func=AF.Sigmoid, bias=gpp[:, d, gi_, kt:kt + 1]),
                            r=[('ps', bank), ('gpp',)], w=[dkey])
                P.add('act', lambda e, d=d, R=R: e.activation(out=A, in_=R, func=AF.Exp, scale=sc8[:, d, 0, kt:kt + 1]), r=[rkey, ('sc8',)], w=[kB[4]])
                P.add('act', lambda e, d=d, R=R: e.activation(out=R, in_=R, func=AF.Exp, scale=sc8[:, d, 1, kt:kt + 1]), r=[rkey, ('sc8',)], w=[rkey])
                P.add('dve', lambda e, R=R: e.tensor_scalar(out=R, in0=R, scalar1=-1.0, scalar2=1.0, op0=ALU.mult, op1=ALU.add), r=[rkey], w=[rkey])
                P.add('act', lambda e, R=R: e.activation(out=R, in_=R, func=AF.Sqrt), r=[rkey], w=[rkey])
                P.add('dve', lambda e, R=R: e.tensor_tensor(out=GI, in0=GI, in1=R, op=ALU.mult), r=[rkey, kB[3]], w=[kB[3]])
                P.add('dve', lambda e, src=src: e.tensor_tensor(out=GI, in0=GI, in1=src, op=ALU.mult), r=[skey, kB[3]], w=[kB[3]])
                P.add('dve', lambda e: e.tensor_scalar(out=sv(A)[:, 1:8, 0:1], in0=sv(A)[:, 1:8, 0:1], scalar1=flags[:, 1:2], scalar2=None, op0=ALU.mult),
                      r=[kB[4], ('flags',)], w=[kB[4]])
                P.add('dve', lambda e, d=d, R=R: e.tensor_tensor_scan(out=R, data0=A, data1=GI, initial=li[:, d, kt:kt + 1], op0=ALU.mult, op1=ALU.add),
                      r=[kB[4], kB[3], ('li',)], w=[rkey])
            n_r0 = len(P.ops)
            reverse(B1, B2, kB[1], kB[2])
            rev_ops = P.ops[n_r0:]
            del P.ops[n_r0:]
            n_f0 = len(P.ops)
            emit_dir(0)
            fwd_ops = P.ops[n_f0:]
            del P.ops[n_f0:]
            merged = [rev_ops[0]]
            ri_, fi_ = 1, 0
            while ri_ < len(rev_ops) or fi_ < len(fwd_ops):
                if fi_ < len(fwd_ops):
                    merged.append(fwd_ops[fi_]); fi_ += 1
                if ri_ < len(rev_ops):
                    merged.append(rev_ops[ri_]); ri_ += 1
            P.ops.extend(merged)
            emit_dir(1)
            reverse(B1, B2, kB[1], kB[2])
            dma(stl_out[:, 0, kt, :], sv(B0)[:, :, 255], [('stl', kt, 0)], r=[kB[0]], stream='stl', slow=True)
            dma(stl_out[:, 1, kt, :], sv(B2)[:, :, 0], [('stl', kt, 1)], r=[kB[2]], stream='stl', slow=True)
            P.add('dve', lambda e: e.tensor_tensor(out=B3, in0=B0, in1=B2, op=ALU.add), r=[kB[0], kB[2]], w=[kB[3]])
            for b in range(NBLK):
                bank = b % 2
                cs = slice(b * BLK, (b + 1) * BLK)
                sl = b % 2
                for k in range(KT):
                    P.add('pe', lambda e, bank=bank, k=k, cs=cs, bg_=bg_: e.matmul(banks[bank][:, :], ld.bf[:, bg_, k, :], hm[:, k, cs],
                                                                                 start=(k == 0), stop=(k == KT - 1)),
                          r=[('lruwbf', bg_), ('hm', k, b)], w=[('ps', bank)])
                P.add('act', lambda e, bank=bank, sl=sl: e.activation(out=gg[:, sl], in_=banks[bank][:, :], func=AF.Square), r=[('ps', bank)], w=[('gg', sl)])
                P.add('dve', lambda e, sl=sl: e.tensor_scalar(out=gg[:, sl], in0=gg[:, sl], scalar1=0.044715, scalar2=1.0, op0=ALU.mult, op1=ALU.add),
                      r=[('gg', sl)], w=[('gg', sl)])
                P.add('dve', lambda e, bank=bank, sl=sl: e.tensor_tensor(out=gg[:, sl], in0=gg[:, sl], in1=banks[bank][:, :], op=ALU.mult),
                      r=[('gg', sl), ('ps', bank)], w=[('gg', sl)])
                P.add('act', lambda e, sl=sl: e.activation(out=gg[:, sl], in_=gg[:, sl], func=AF.Sigmoid, scale=1.5957691216), r=[('gg', sl)], w=[('gg', sl)])
                P.add('dve', lambda e, bank=bank, sl=sl: e.tensor_tensor(out=gg[:, sl], in0=gg[:, sl], in1=banks[bank][:, :], op=ALU.mult),
                      r=[('gg', sl), ('ps', bank)], w=[('gg', sl)])
                P.add('dve', lambda e, sl=sl, cs=cs, kt=kt: e.tensor_tensor(out=yfm[:, kt, cs], in0=gg[:, sl], in1=B3[:, cs], op=ALU.mult),
                      r=[('gg', sl), kB[3]], w=[('yfm', kt)])
        for kt_ in range(KT):
            lru_kt(kt_)
        gt = gtv[(layer, sub)]
        pc = 0
        for m in range(KT):
            bw_ = ld.load(wsrc(lru_w_out, m * 128))
            for b in range(NBLK):
                bank = 4 + (pc % 2)
                pc += 1
                cs = slice(b * BLK, (b + 1) * BLK)
                for k in range(KT):
                    P.add('pe', lambda e, bank=bank, k=k, cs=cs, bw_=bw_: e.matmul(banks[bank][:, :], ld.bf[:, bw_, k, :], yfm[:, k, cs],
                                                                                 start=(k == 0), stop=(k == KT - 1)),
                          r=[('lruwbf', bw_), ('yfm', k)], w=[('ps', bank)])
                P.add('dve', lambda e, bank=bank, m=m, cs=cs, b=b: e.scalar_tensor_tensor(
                    out=xs[:, m, cs], in0=banks[bank][:, :], scalar=gt[:, m:m + 1], in1=xs[:, m, cs], op0=ALU.mult, op1=ALU.add),
                    r=[('ps', bank), ('xs', m, b), ('sm', 'gt', layer, sub)], w=[('xs', m, b)])
        P.barrier(dummy[:])
        for b in range(NBLK):
            layernorm(b, ln_idx, 0)


    env = dict(locals())
    P.barrier(dummy[:])
    def ada1_hook(f):
        for half in range(2):
            ch = f * 2 + half
            if ch < 36:
                ada_compute(1, ch, 256, 23200, 'adaB')
            if ch + 2 < 36:
                ada_load(1, ch + 2, 256, 23200, 'adaB')
    if STAGE >= 1:
        ada_load(1, 0, 256, 23200, 'adaB')
        ada_load(1, 1, 256, 23200, 'adaB')
        ffn(0, 0, 0, hook=ada1_hook)
        derive(1)
        P.barrier(dummy[:])
    if STAGE >= 2:
        s5_mixer()
        P.barrier(dummy[:])
    if STAGE >= 3:
        ffn(0, 2, 2)
        ffn(1, 0, 3, soft=True)
    if STAGE >= 4:
        lru_mixer()
    if STAGE >= 5:
        ffn(1, 2, 5)
    if STAGE < 99:
        for kt in range(KT):
            dma(y_out[kt * 128:(kt + 1) * 128, :], xs[:, kt, :], [('yout', kt)], r=[('xs', kt, b) for b in range(NBLK)], stream='yo%d' % (kt % 4))
    return nc, P, ctx


def finalize(nc, P, ctx):
    P.analyze()
    names = set(s for s in P.final.keys())
    sems = {}
    for s in sorted(names):
        c = nc.semaphore(s)
        ctx.append(c)
        sems[s] = c.__enter__()
    out_streams = [s for s in P.streams if s.startswith('dma_yo') or s.startswith('dma_st')]
    with nc.Block() as block:
        @block.tensor
        def _(e):
            P.emit('pe', e, sems)

        @block.scalar
        def _(e):
            P.emit('act', e, sems)

        @block.vector
        def _(e):
            P.emit('dve', e, sems)

        @block.gpsimd
        def _(e):
            P.emit('pool', e, sems)

        @block.sync
        def _(e):
            P.emit('sp', e, sems, out_streams)
    for c in reversed(ctx):
        c.__exit__(None, None, None)
    return nc


def host_inputs(inp):
    f = lambda a: np.ascontiguousarray(np.asarray(a, dtype=np.float32))
    xp = f(inp["x_prompt"]); xsmp = f(inp["x_sample"])
    c = f(inp["c"]); c_ctx = f(inp["c_ctx"])
    consts = np.zeros((128, 128 * 3 + 1024), np.float32)
    consts[:, 0:128] = np.eye(128)
    consts[:, 128:256] = 1.0
    consts[:, 256:384] = np.eye(128)[::-1]
    k = np.arange(128) // 16
    for kh in range(2):
        kk = kh * 8 + k
        tau = np.arange(256) // 16
        consts[:, 384 + kh * 256:384 + (kh + 1) * 256] = (tau[None, :] >= kk[:, None])
        consts[:, 896 + kh * 256:896 + (kh + 1) * 256] = (kk[:, None] >= tau[None, :])
    shared = {}
    shared["consts_in"] = consts
    shared["ada_w"] = f(inp["ada_w"])
    shared["ada_b"] = f(f(inp["ada_b"]).reshape(2, 72, 128).transpose(2, 0, 1))
    shared["ln_g"] = f(f(inp["ln_g"]).reshape(2, 3, 8, 128).transpose(3, 0, 1, 2))
    shared["ln_b"] = f(f(inp["ln_b"]).reshape(2, 3, 8, 128).transpose(3, 0, 1, 2))
    shared["w_up"] = f(inp["ffn_w_up"])
    shared["w_down"] = f(inp["ffn_w_down"])
    shared["s5_w_in"] = f(inp["s5_w_in"][0])
    shared["s5_w_glu"] = f(inp["s5_w_glu"][0])
    are = f(inp["s5_a_re"])[0]; aim = f(inp["s5_a_im"])[0]; ldt = f(inp["s5_log_dt"])[0]
    par = np.zeros((2, 64, 64, 3), np.float32)
    par[..., 0] = are; par[..., 1] = aim; par[..., 2] = ldt[:, :, None]
    par = par.reshape(2, 32, 2, 64, 3).transpose(2, 3, 0, 1, 4).reshape(128, 2, 32, 3)
    shared["s5_par"] = f(par)
    bre = f(inp["s5_b_re"])[0]; bim = f(inp["s5_b_im"])[0]
    cre = f(inp["s5_c_re"])[0].transpose(0, 2, 1); cim = f(inp["s5_c_im"])[0].transpose(0, 2, 1)
    bc = np.stack([bre, bim, cre, cim], 0)
    bc = bc.reshape(4, 32, 2, 64, 16).transpose(2, 3, 0, 1, 4).reshape(128, 4, 32, 16)
    shared["s5_bc"] = f(bc)
    shared["s5_dsk"] = f(np.broadcast_to(f(inp["s5_d"])[0][None, :], (128, D)))
    shared["lru_w_in"] = f(inp["lru_w_in"][0])
    shared["lru_w_out"] = f(inp["lru_w_out"][0])
    cw = f(inp["lru_conv_w"])[0]; cb = f(inp["lru_conv_b"])[0]
    conv = np.concatenate([cw, cb[None]], 0).reshape(5, 8, 128).transpose(2, 0, 1)
    shared["lru_conv"] = f(conv)
    wa = f(inp["lru_w_a"])[0]; wx = f(inp["lru_w_x"])[0]
    gw = np.zeros((128, 4, 8, 128), np.float32)
    for d in range(2):
        for gi_, wsrc in enumerate((wa, wx)):
            for h in range(16):
                kt, hp = h // 2, h % 2
                gw[hp * 64:(hp + 1) * 64, d * 2 + gi_, kt, hp * 64:(hp + 1) * 64] = wsrc[d, h]
    shared["lru_gw"] = gw
    ba = f(inp["lru_b_a"])[0]; bx = f(inp["lru_b_x"])[0]; lam = f(inp["lru_lambda"])[0]
    gp_ = np.stack([ba, bx, lam], 1).reshape(2, 3, 8, 128).transpose(3, 0, 1, 2)
    shared["lru_gp"] = f(gp_)
    st5 = f(inp["state_s5"]); stl = f(inp["state_lru"])
    maps = []
    for core in range(8):
        m = dict(shared)
        if core < 4:
            xc = xp[core * 8:(core + 1) * 8].reshape(T, D)
            cond = c_ctx
            fl = [0.0, 0.0, 0.0, 0.0]
            mk = np.ones(256, np.float32)
            mk[[16, 32, 48, 64, 80, 96, 112]] = 0.0
            mk[[128 + 15, 128 + 31, 128 + 47, 128 + 63, 128 + 79, 128 + 95, 128 + 111]] = 0.0
            s5i = np.zeros((128, 2, 32, 2), np.float32)
            lri = np.zeros((128, 2, 8), np.float32)
        else:
            bi = core - 4
            xc = xsmp[bi]
            cond = c[bi]
            fl = [1.0, 1.0, 0.0, 0.0]
            mk = np.ones(256, np.float32)
            s = st5[bi, 0]
            s5i = s.reshape(2, 2, 32, 2, 64).transpose(3, 4, 1, 2, 0).reshape(128, 2, 32, 2)
            lri = stl[bi, 0].reshape(2, 8, 128).transpose(2, 0, 1)
        m["x_in"] = f(xc.T)
        m["cond_in"] = f(cond.reshape(8, 128).T)
        m["flags_in"] = f(np.broadcast_to(np.array(fl, np.float32)[None], (128, 4)))
        m["maskdc_in"] = f(np.broadcast_to(mk[None], (128, 256)))
        m["s5init_in"] = f(s5i)
        m["lruinit_in"] = f(lri)
        maps.append(m)
    return maps


_CACHE = {}


def kernel(**inputs):
    maps = host_inputs(inputs)
    if "nc" not in _CACHE:
        nc, P, ctx = build()
        _CACHE["nc"] = finalize(nc, P, ctx)
    nc = _CACHE["nc"]
    used = set(t for t in maps[0].keys())
    res = run_bass_kernel_spmd(nc, maps, core_ids=list(range(8)))
    R = res.results
    y_p = np.stack([R[cidx]["y_out"].T.reshape(8, 256, D) for cidx in range(4)], 0).reshape(32, 256, D)
    y_s = np.stack([R[cidx]["y_out"].T for cidx in range(4, 8)], 0)
    s5 = np.zeros((32, 1, 2, 2, 64, 64), np.float32)
    sl = np.zeros((32, 1, 2, 1024), np.float32)
    for cidx in range(4):
        a = R[cidx]["st5_out"]
        a = a.reshape(2, 64, 8, 2, 4, 2, 8)
        a = a.transpose(6, 5, 3, 2, 4, 0, 1)
        s5[cidx * 8:(cidx + 1) * 8, 0] = a.reshape(8, 2, 2, 64, 64)
        b = R[cidx]["stl_out"]
        sl[cidx * 8:(cidx + 1) * 8, 0] = b.transpose(3, 1, 2, 0).reshape(8, 2, 1024)
    return (y_p.astype(np.float32), y_s.astype(np.float32), s5, sl)
```
